# Optimizing a Trainium2 kernel written in Bass

```python
import jax, jax.numpy as jnp
from jax import lax
import numpy as np

D_MODEL = 1024
BATCH = 8
SEQ = 2048
DEPTH = 2
DEC_BATCH = 128
DEC_SEQ = 1
PAST_LEN = 16384
PAGE_SIZE = 128

MIX = 2 * D_MODEL
RET_HEADS = 8
RET_WIDTH = MIX // 2
RET_DK = RET_WIDTH // RET_HEADS
RET_DV = RET_WIDTH // RET_HEADS
GM_HEADS = 8
GM_WIDTH = MIX - RET_WIDTH
GM_DIM = GM_WIDTH // GM_HEADS
CHUNK = 128
ROPE_BASE = 10000.0
EPS = 1e-6
SPLITS = [RET_WIDTH, 2 * RET_WIDTH, 3 * RET_WIDTH, 4 * RET_WIDTH,
          4 * RET_WIDTH + GM_WIDTH, 4 * RET_WIDTH + 2 * GM_WIDTH]
IN_COLS = 4 * RET_WIDTH + 3 * GM_WIDTH

kernel_name = "hybrid_retention_gmlp_decoder_step"


def _rmsnorm(x, g):
    xf = x.astype(jnp.float32)
    y = xf * lax.rsqrt(jnp.mean(xf * xf, axis=-1, keepdims=True) + EPS) * g.astype(jnp.float32)
    return y.astype(x.dtype)


def _layernorm(x, g, b):
    xf = x.astype(jnp.float32)
    mu = jnp.mean(xf, axis=-1, keepdims=True)
    var = jnp.mean(jnp.square(xf - mu), axis=-1, keepdims=True)
    y = (xf - mu) * lax.rsqrt(var + EPS) * g.astype(jnp.float32) + b.astype(jnp.float32)
    return y.astype(x.dtype)


def _rope(x, pos):
    d = x.shape[-1]
    inv = ROPE_BASE ** (-jnp.arange(0, d, 2, dtype=jnp.float32) / d)
    ang = pos[:, None] * inv[None, :]
    c = jnp.cos(ang)[:, None, :]
    s = jnp.sin(ang)[:, None, :]
    xf = x.astype(jnp.float32)
    x1, x2 = xf[..., : d // 2], xf[..., d // 2:]
    return jnp.concatenate([x1 * c - x2 * s, x1 * s + x2 * c], axis=-1)


def _log_gamma():
    return jnp.log(1.0 - 2.0 ** (-5.0 - jnp.arange(RET_HEADS, dtype=jnp.float32)))


def _retention(q, k, v, S0):
    B, T, H, _ = q.shape
    c = CHUNK if T % CHUNK == 0 else T
    n = T // c
    lg = _log_gamma()
    idx = jnp.arange(c, dtype=jnp.float32)
    diff = idx[:, None] - idx[None, :]
    decay = jnp.where(diff >= 0, jnp.exp(lg[:, None, None] * jnp.maximum(diff, 0.0)), 0.0)
    q_dec = jnp.exp(lg[None, :] * (idx[:, None] + 1.0))
    k_dec = jnp.exp(lg[None, :] * (c - 1.0 - idx[:, None]))
    s_dec = jnp.exp(lg * c)

    def to_chunks(a):
        return a.astype(jnp.float32).reshape(B, n, c, H, -1).transpose(1, 0, 2, 3, 4)

    def body(S, blk):
        qc, kc, vc = blk
        sc = jnp.einsum('bihd,bjhd->bhij', qc, kc) * decay
        o = (jnp.einsum('bhij,bjhv->bihv', sc, vc)
             + jnp.einsum('bihd,bhdv->bihv', qc, S) * q_dec[None, :, :, None])
        S = S * s_dec[None, :, None, None] + jnp.einsum('bjhd,bjhv->bhdv', kc * k_dec[None, :, :, None], vc)
        return S, o

    S, o = lax.scan(body, S0.astype(jnp.float32), (to_chunks(q), to_chunks(k), to_chunks(v)))
    o = o.transpose(1, 0, 2, 3, 4).reshape(B, T, H, -1)
    return o, S


def _spatial_gate(v, ws, bs):
    B, T, GH, GD = v.shape
    c = CHUNK if T % CHUNK == 0 else T
    n = T // c
    w = jnp.tril(ws[:, :c, :c])
    b = bs[:, :c].T
    vc = v.reshape(B, n, c, GH, GD)
    s = jnp.einsum('hij,bnjhg->bnihg', w, vc) + b[None, None, :, :, None]
    return s.reshape(B, T, GH, GD)


def _layer(h, pos, S0, norm_g, w_in, w_out, ws, bs, ln_g, ln_b):
    B, T, _ = h.shape
    xn = _rmsnorm(h, norm_g)
    proj = jnp.einsum('btd,de->bte', xn, w_in)
    q, k, v, g_r, u, v_g, g_g = jnp.split(proj, SPLITS, axis=-1)
    q = _rope(q.reshape(B, T, RET_HEADS, RET_DK), pos)
    k = _rope(k.reshape(B, T, RET_HEADS, RET_DK), pos) * (RET_DK ** -0.5)
    v = v.reshape(B, T, RET_HEADS, RET_DV)
    o, S = _retention(q, k, v, S0)
    o = o * lax.rsqrt(jnp.mean(o * o, axis=-1, keepdims=True) + EPS)
    o = o.reshape(B, T, RET_WIDTH).astype(h.dtype) * jax.nn.silu(g_r)
    vn = _layernorm(v_g, ln_g, ln_b).reshape(B, T, GM_HEADS, GM_DIM)
    s = _spatial_gate(vn, ws, bs).reshape(B, T, GM_WIDTH)
    m = u * s.astype(h.dtype) * jax.nn.silu(g_g)
    y = jnp.einsum('bte,ed->btd', jnp.concatenate([o, m], axis=-1), w_out)
    return h + y, S.astype(S0.dtype), vn


def setup_inputs(seed: int = 0) -> dict:
    key = jax.random.key(seed)
    ks = jax.random.split(key, 12)
    f32 = jnp.float32
    return {
        "x_prompt": jax.random.normal(ks[0], (BATCH, SEQ, D_MODEL), f32),
        "x_sample": jax.random.normal(ks[1], (DEC_BATCH, DEC_SEQ, D_MODEL), f32),
        "state_ret": 0.5 * jax.random.normal(ks[2], (DEPTH, DEC_BATCH, RET_HEADS, RET_DK, RET_DV), f32),
        "norm_g": 1.0 + 0.02 * jax.random.normal(ks[3], (DEPTH, D_MODEL), f32),
        "w_in": jax.random.normal(ks[4], (DEPTH, D_MODEL, IN_COLS), f32) * D_MODEL ** -0.5,
        "w_out": jax.random.normal(ks[5], (DEPTH, MIX, D_MODEL), f32) * MIX ** -0.5,
        "gm_ws": jax.random.normal(ks[6], (DEPTH, GM_HEADS, CHUNK, CHUNK), f32) * CHUNK ** -0.5,
        "gm_b": 1.0 + 0.02 * jax.random.normal(ks[7], (DEPTH, GM_HEADS, CHUNK), f32),
        "gm_ln_g": 1.0 + 0.02 * jax.random.normal(ks[8], (DEPTH, GM_WIDTH), f32),
        "gm_ln_b": 0.02 * jax.random.normal(ks[9], (DEPTH, GM_WIDTH), f32),
        "final_g": 1.0 + 0.02 * jax.random.normal(ks[10], (D_MODEL,), f32),
    }


def reference(x_prompt, x_sample, state_ret, norm_g, w_in, w_out, gm_ws, gm_b, gm_ln_g, gm_ln_b, final_g):
    pos_p = jnp.arange(SEQ, dtype=jnp.float32)
    pos_s = PAST_LEN + jnp.arange(DEC_SEQ, dtype=jnp.float32)
    s0_prompt = jnp.zeros((BATCH, RET_HEADS, RET_DK, RET_DV), state_ret.dtype)
    h_p, h_s = x_prompt, x_sample
    sp_list, ss_list, vs_list = [], [], []
    for l in range(DEPTH):
        h_p, S_p, _ = _layer(h_p, pos_p, s0_prompt, norm_g[l], w_in[l], w_out[l],
                             gm_ws[l], gm_b[l], gm_ln_g[l], gm_ln_b[l])
        h_s, S_s, v_s = _layer(h_s, pos_s, state_ret[l], norm_g[l], w_in[l], w_out[l],
                               gm_ws[l], gm_b[l], gm_ln_g[l], gm_ln_b[l])
        sp_list.append(S_p)
        ss_list.append(S_s)
        vs_list.append(v_s)
    y_prompt = _rmsnorm(h_p, final_g)
    y_sample = _rmsnorm(h_s, final_g)
    state_ret_prompt = jnp.stack(sp_list)
    state_ret_sample = jnp.stack(ss_list)
    state_gm_v_sample = jnp.stack(vs_list)
    return (y_prompt, y_sample, state_ret_prompt, state_ret_sample, state_gm_v_sample)
```

```python
import numpy as np
from contextlib import ExitStack
import concourse.bass as bass
import concourse.mybir as mybir
from concourse.bass_utils import run_bass_kernel_spmd

F32 = mybir.dt.float32
BF16 = mybir.dt.bfloat16
AF = mybir.ActivationFunctionType
ALU = mybir.AluOpType

D = 1024
SEQ = 2048
NCH = 16
DEPTH = 2
NS = 16
H = 8
PAST = 16384
EPS = 1e-6
NCORES = 8


class _Op:
    __slots__ = ("eng", "fn", "deps", "dma", "semkey", "sig", "waits", "r", "w")

    def __init__(self, eng, fn, deps, dma, semkey):
        self.eng, self.fn, self.deps, self.dma, self.semkey = eng, fn, deps, dma, semkey
        self.sig = None
        self.waits = []


class Prog:
    def __init__(self, nc):
        self.nc = nc
        self.ops = []
        self.last_w = {}
        self.readers = {}

    def add(self, eng, fn, r=(), w=(), dma=False, semkey=None):
        i = len(self.ops)
        deps = {}
        for k in r:
            j = self.last_w.get(k)
            if j is not None:
                deps[j] = True
        for k in w:
            j = self.last_w.get(k)
            if j is not None:
                deps.setdefault(j, False)
            for j in self.readers.get(k, ()):
                deps.setdefault(j, False)
        for k in r:
            self.readers.setdefault(k, []).append(i)
        for k in w:
            self.last_w[k] = i
            self.readers[k] = []
        if dma:
            assert semkey is not None
        op = _Op(eng, fn, deps, dma, semkey)
        op.r, op.w = list(r), list(w)
        self.ops.append(op)
        return i

    def emit(self, stack):
        nc = self.nc
        ops = self.ops
        for i, op in enumerate(ops):
            for j, raw in op.deps.items():
                p = ops[j]
                if p.dma:
                    need = True
                elif p.eng == op.eng:
                    if op.dma:
                        need = True
                    elif p.eng == "pe":
                        need = False
                    else:
                        need = True
                else:
                    need = True
                if need:
                    op.waits.append(j)
                    p.sig = True
        cnt = {}
        sems = {}
        for op in ops:
            if not op.sig:
                continue
            k = ("d", op.semkey) if op.dma else ("e", op.eng)
            cnt[k] = cnt.get(k, 0) + (16 if op.dma else 1)
            op.sig = (k, cnt[k])
            sems[k] = None
        for n, k in enumerate(sems):
            sems[k] = stack.enter_context(nc.semaphore("sem%d" % n))
        self.nsems = len(sems)
        block = stack.enter_context(nc.Block())

        def replay(engname, e):
            waited = {}
            for op in ops:
                if op.eng != engname:
                    continue
                need = {}
                for j in op.waits:
                    k, v = ops[j].sig
                    if waited.get(k, 0) >= v:
                        continue
                    need[k] = max(need.get(k, 0), v)
                for k, v in need.items():
                    e.wait_ge(sems[k], v)
                    waited[k] = v
                inst = op.fn(e)
                if op.sig:
                    inst.then_inc(sems[op.sig[0]], 16 if op.dma else 1)

        @block.sync
        def _(e):
            replay("sp", e)

        @block.tensor
        def _(e):
            replay("pe", e)

        @block.scalar
        def _(e):
            replay("act", e)

        @block.vector
        def _(e):
            replay("dve", e)

        @block.gpsimd
        def _(e):
            replay("pool", e)


def _consts():
    lg = np.log(1.0 - 2.0 ** (-5.0 - np.arange(H, dtype=np.float64)))
    idx = np.arange(128, dtype=np.float64)
    qdec = np.exp(lg[None, :] * (idx[:, None] + 1.0))
    kinv = np.exp(-lg[None, :] * (idx[:, None] + 1.0)) * (128.0 ** -0.5)
    sdec = np.exp(lg * 128.0)
    gam = np.exp(lg)
    inv = (10000.0 ** (-np.arange(0, 128, 2, dtype=np.float32) / np.float32(128))).astype(np.float32)
    pos = np.arange(SEQ, dtype=np.float32)
    ang = (pos[:, None] * inv[None, :]).astype(np.float32).astype(np.float64)
    ropeP = np.stack([np.cos(ang), np.sin(ang), -np.sin(ang)], axis=1)
    ropeP = ropeP.reshape(NCH, 128, 3, 64).astype(np.float32)
    angs = (np.float32(PAST) * inv).astype(np.float32).astype(np.float64)
    ropeS = np.stack([np.cos(angs), np.sin(angs), -np.sin(angs)], axis=0)[None].repeat(NS, 0).astype(np.float32)
    maskT = (idx[:, None] <= idx[None, :]).astype(np.float32)
    ident = np.eye(128, dtype=np.float32)
    gamP = np.repeat(gam, 16).astype(np.float32).reshape(128, 1)
    blockind = (np.arange(128)[:, None] // 16 == np.arange(H)[None, :]).astype(np.float32)
    return dict(qdec=qdec.astype(np.float32), kinv=kinv.astype(np.float32), sdec=[float(x) for x in sdec],
                gam=[float(x) for x in gam], ropeP=ropeP, ropeS=ropeS, maskT=maskT, ident=ident, gamP=gamP, blockind=blockind)


_C = _consts()


def build_nc():
    nc = bass.Bass("TRN2", target_bir_lowering=False)
    dt_in = lambda name, shape: nc.dram_tensor(name, shape, F32, kind="ExternalInput").ap()
    dt_out = lambda name, shape: nc.dram_tensor(name, shape, F32, kind="ExternalOutput").ap()
    xp = dt_in("xp", [SEQ, D])
    xs_in = dt_in("xs", [NS, D])
    st_in = dt_in("st", [DEPTH, NS, H, 128, 128])
    norm_g = dt_in("norm_g", [DEPTH, D])
    w_in = dt_in("w_in", [DEPTH, D, 7 * D])
    w_out = dt_in("w_out", [DEPTH, 2 * D, D])
    gm_ws = dt_in("gm_ws", [DEPTH, H, 128, 128])
    gm_b = dt_in("gm_b", [DEPTH, H, 128])
    ln_g = dt_in("ln_g", [DEPTH, D])
    ln_b = dt_in("ln_b", [DEPTH, D])
    fin_g = dt_in("fin_g", [D])
    c_qdec = dt_in("c_qdec", [128, H])
    c_kinv = dt_in("c_kinv", [128, H])
    c_ropeP = dt_in("c_ropeP", [NCH, 128, 3, 64])
    c_ropeS = dt_in("c_ropeS", [NS, 3, 64])
    c_maskT = dt_in("c_maskT", [128, 128])
    c_ident = dt_in("c_ident", [128, 128])
    c_gamP = dt_in("c_gamP", [128, 1])
    c_blockind = dt_in("c_blockind", [128, H])
    yp = dt_out("yp", [SEQ, D])
    ys = dt_out("ys", [NS, D])
    sp_out = dt_out("sp_out", [DEPTH, H, 128, 128])
    ss_out = dt_out("ss_out", [DEPTH, NS, H, 128, 128])
    gv_out = dt_out("gv_out", [DEPTH, NS, D])
    h1 = nc.dram_tensor("h1", [SEQ + NS, D], F32, kind="Internal").ap()
    qkvscr = nc.dram_tensor("qkvscr", [3, NS, D], BF16, kind="Internal").ap()
    oscr = nc.dram_tensor("oscr", [NS, D], F32, kind="Internal").ap()
    wbf_in = nc.dram_tensor("wbf_in", [D, 7 * D], BF16, kind="Internal").ap()
    wbf_out = nc.dram_tensor("wbf_out", [2 * D, D], BF16, kind="Internal").ap()

    with ExitStack() as st:
        sb = lambda name, shape, dt: st.enter_context(nc.sbuf_tensor(name, shape, dt))
        wg = [sb("wg%d" % g, [128, 8, 1024], BF16) for g in range(7)]
        wo = [sb("wo%d" % g, [128, 8, 1024], BF16) for g in range(2)]
        identf = sb("identf", [128, 128], F32)
        identb = sb("identb", [128, 128], BF16)
        maskT = sb("maskT", [128, 128], F32)
        qdec = sb("qdec", [128, H], F32)
        kinv = sb("kinv", [128, H], F32)
        gamP = sb("gamP", [128, 1], F32)
        blockind = sb("blockind", [128, H], F32)
        rtab = sb("rtab", [128, 3, 64], F32)
        WT = sb("WT", [128, H, 128], BF16)
        ngT = sb("ngT", [128, 8], F32)
        bsT = sb("bsT", [128, H], F32)
        w00 = sb("w00", [NS, H], F32)
        b00 = sb("b00", [NS, H], F32)
        lng = sb("lng", [128, D], F32)
        lnb = sb("lnb", [128, D], F32)
        fing = sb("fing", [128, D], F32)
        hbuf = [sb("hbuf%d" % i, [128, D], F32) for i in range(3)]
        tmpB = sb("tmpB", [128, D], F32)
        S32 = sb("S32", [128, D], F32)
        xnT = sb("xnT", [128, 8, 128], BF16)
        q_rot = sb("q_rot", [128, D], BF16)
        k_rot = sb("k_rot", [128, D], BF16)
        vb = [sb("vb%d" % i, [128, D], BF16) for i in range(1)]
        vn = sb("vn", [128, D], BF16)
        sg = [sb("sg%d" % i, [128, D], BF16) for i in range(1)]
        sgg = sb("sgg", [128, D], BF16)
        Sbf = sb("Sbf", [128, H, 128], BF16)
        cat = sb("cat", [128, 2 * D], BF16)
        catT = sb("catT", [128, 16, 128], BF16)
        k2all = sb("k2all", [128, NS, 8], BF16)
        q2all = sb("q2all", [128, NS, 8], BF16)
        vb2 = sb("vb2", [128, 8, 128], BF16)
        Q2 = [sb("Q2_%d" % i, [128, 8, H], BF16) for i in range(2)]
        osbt = sb("osbt", [128, 128], F32)
        osb = [osbt[0:H, :]]
        junk = osbt[:, 0:64].bitcast(BF16)
        bst = sb("bst", [128, 12], F32)
        mv = sb("mv", [128, 2], F32)
        oss = sb("oss", [128, H], F32)
        orstd = sb("orstd", [128, H], F32)
        st1 = sb("st1", [128, 8], F32)
        ssq, ssq2, rstd2, rstd, lsd, nmr, epst = [st1[:, i:i + 1] for i in range(7)]
        mhalf = st1[:, 7:8]
        ps = st.enter_context(nc.psum_tensor("ps", [128, 4, 1024], F32))

        P = Prog(nc)
        from collections import deque
        freep = {0: (-4, 0), 1: (-3, 0), 2: (-2, 0), 3: (-1, 0)}
        atick = [0]

        def alloc():
            while not freep:
                yield "blocked"
            atick[0] += 1
            p = min(freep, key=lambda k: freep[k][0] + freep[k][1])
            del freep[p]
            return p

        def free(p, late=False):
            freep[p] = (atick[0], 2 if late else 0)

        vb2keys = [("vb2", h, g0) for h in range(H) for g0 in range(2)]
        vb2f = vb2[:, :, :].rearrange("p g v -> p (g v)").bitcast(F32)
        qT32 = vb2f[:, 0:256].rearrange("p (h t) -> p h t", h=H)
        kT32 = vb2f[:, 256:512].rearrange("p (h t) -> p h t", h=H)

        def PSb(p):
            return ps[:, p, :].bitcast(BF16)

        xs = tmpB[:, 0:512].bitcast(BF16)
        scT = catT[:, 0:8, :]
        qdT = cat[:, 0:D].rearrange("p (h t) -> p h t", h=H)
        kkT = cat[:, D:2 * D].rearrange("p (h t) -> p h t", h=H)
        catf = cat[:, :].bitcast(F32)
        junk2 = catT[:, 0:8, :].rearrange("p c t -> p (c t)")

        def ld(dst, src, key, eng="sp", **kw):
            P.add(eng, lambda e: e.dma_start(out=dst, in_=src, **kw), w=[key], dma=True, semkey=key)

        ld(identf[:], c_ident, "identf")
        ld(maskT[:], c_maskT, "maskT")
        ld(qdec[:], c_qdec, "qdec")
        ld(kinv[:], c_kinv, "kinv")
        ld(gamP[:], c_gamP, "gamP")
        ld(blockind[:], c_blockind, "blockind")
        P.add("dve", lambda e: e.memset(epst[:], EPS), w=["epst"])
        P.add("dve", lambda e: e.memset(mhalf[:], -0.5), w=["mhalf"])
        P.add("dve", lambda e: e.tensor_copy(identb[:], identf[:]), r=["identf"], w=["identb"])

        def precast_weights(l):
            for g in range(7):
                P.add("pool", (lambda g: lambda e: e.dma_start(out=wbf_in[:, g * 1024:(g + 1) * 1024],
                                                               in_=w_in[l, :, g * 1024:(g + 1) * 1024]))(g),
                      w=[("wbf", g)], dma=True, semkey=("wbf", g))
            for g in range(2):
                P.add("pool", (lambda g: lambda e: e.dma_start(out=wbf_out[g * 1024:(g + 1) * 1024, :],
                                                               in_=w_out[l, g * 1024:(g + 1) * 1024, :]))(g),
                      w=[("wbfo", g)], dma=True, semkey=("wbfo", g))

        def load_weights_bf(l, which="all"):
            for g in (range(7) if which in ("all", "in") else []):
                src = wbf_in[:, g * 1024:(g + 1) * 1024].rearrange("(c p) n -> p c n", p=128)
                P.add("pool", (lambda g, src: lambda e: e.dma_start(out=wg[g][:], in_=src))(g, src),
                      r=[("wbf", g)], w=[("wg", g)], dma=True, semkey=("wg", g))
            for g in (range(2) if which in ("all", "out") else []):
                src = wbf_out[g * 1024:(g + 1) * 1024, :].rearrange("(c p) n -> p c n", p=128)
                P.add("pool", (lambda g, src: lambda e: e.dma_start(out=wo[g][:], in_=src))(g, src),
                      r=[("wbfo", g)], w=[("wo", g)], dma=True, semkey=("wo", g))

        def load_weight_group(l, g):
            if g < 7:
                src = w_in[l, :, g * 1024:(g + 1) * 1024].rearrange("(c p) n -> p c n", p=128)
                P.add("pool", lambda e: e.dma_start(out=wg[g][:], in_=src), w=[("wg", g)], dma=True, semkey=("wg", g))
            else:
                src = w_out[l, (g - 7) * 1024:(g - 6) * 1024, :].rearrange("(c p) n -> p c n", p=128)
                P.add("pool", lambda e: e.dma_start(out=wo[g - 7][:], in_=src), w=[("wo", g - 7)], dma=True,
                      semkey=("wo", g - 7))

        deferred_w = []

        def load_weights(l):
            for g in (0, 1, 2):
                load_weight_group(l, g)
            deferred_w.extend([(l, g) for g in (3, 6, 4, 5, 7, 8)])

        def sample_input_loads(l):
            src = xs_in if l == 0 else h1[SEQ:SEQ + NS, :]
            P.add("sp", lambda e: e.dma_start(out=hbuf[0][0:NS, :], in_=src), r=[("h1", "s")] if l else [],
                  w=[("hbuf", 0)], dma=True, semkey=("hbuf", 0))
            P.add("sp", lambda e: e.dma_start(out=rtab[0:NS, :, :], in_=c_ropeS), w=["rtab"], dma=True, semkey="rtab")
            ld(ngT[:], norm_g[l].rearrange("(c p) -> p c", p=128), "ngT", allow_slow_non_contiguous=True)

        def small_loads(l, stg, stgkey):
            ld(bsT[:], gm_b[l].rearrange("h i -> i h"), "bsT", allow_slow_non_contiguous=True)
            ld(w00[:], gm_ws[l, :, 0, 0].partition_broadcast(NS), "w00", allow_slow_non_contiguous=True)
            ld(b00[:], gm_b[l, :, 0].partition_broadcast(NS), "b00", allow_slow_non_contiguous=True)
            ld(lng[:], ln_g[l].partition_broadcast(128), "lng")
            ld(lnb[:], ln_b[l].partition_broadcast(128), "lnb")
            if stg is not None:
                stage_ws(l, stg, stgkey)

        def stage_ws(l, stg, stgkey):
            keys = stgkey if isinstance(stgkey, list) else [stgkey]
            P.add("sp", lambda e: e.dma_start(out=stg[:].rearrange("p (h j) -> p h j", h=H),
                                              in_=gm_ws[l].rearrange("h i j -> i h j")),
                  w=keys, dma=True, semkey=keys[0])

        def wt_prep(l, stg, stgkey):
            p = yield from alloc()

            def tr(e):
                for h in range(H):
                    i = e.transpose(ps[:, p, h * 128:(h + 1) * 128], stg[:, h * 128:(h + 1) * 128], identf[:])
                return i
            yield P.add("pe", tr, r=(stgkey if isinstance(stgkey, list) else [stgkey]) + ["identf"], w=[("ps", p)])
            yield P.add("dve", lambda e: e.tensor_tensor(
                out=WT[:], in0=ps[:, p, :].rearrange("p (h i) -> p h i", h=H),
                in1=maskT[:].unsqueeze(1).to_broadcast([128, H, 128]), op=ALU.mult),
                r=[("ps", p), "maskT"], w=["WT"])
            free(p)

        def rmsnorm_to_xnT(n, hb, hkey, split=False):
            yield P.add("act", lambda e: e.activation(out=xs[0:n, :], in_=hb[0:n, :], func=AF.Square, accum_out=ssq[0:n, :]),
                        r=[hkey], w=["tmpB", "ssq"])
            yield P.add("dve", lambda e: e.tensor_scalar(out=ssq[0:n, :], in0=ssq[0:n, :], scalar1=1.0 / D, scalar2=EPS,
                                                         op0=ALU.mult, op1=ALU.add), r=["ssq"], w=["ssq"])
            yield P.add("pool", lambda e: e.tensor_tensor(out=rstd[0:n, :], in0=ssq[0:n, :], in1=mhalf[0:n, :], op=ALU.pow),
                        r=["ssq", "mhalf"], w=["rstd"])
            yield P.add("act", lambda e: e.activation(out=xs[0:n, :], in_=hb[0:n, :], func=AF.Copy, scale=rstd[0:n, :]),
                        r=[hkey, "rstd"], w=["tmpB"])
            if split:
                yield "seg"
            p = yield from alloc()
            pb = PSb(p)

            def tr(e):
                for c in range(8):
                    i = e.transpose(pb[:, c * 128:c * 128 + n], xs[0:n, c * 128:(c + 1) * 128], identb[0:n, 0:n])
                return i
            yield P.add("pe", tr, r=["tmpB", "identb"], w=[("ps", p)])
            yield P.add("dve", lambda e: e.tensor_tensor(
                out=xnT[:, :, 0:n], in0=pb[:, 0:1024].rearrange("p (c t) -> p c t", c=8)[:, :, 0:n],
                in1=ngT[:].unsqueeze(2).to_broadcast([128, 8, n]), op=ALU.mult),
                r=[("ps", p), "ngT"], w=["xnT"])
            free(p)

        def proj(g, n):
            p = yield from alloc()

            def mm(e):
                for half in range(2):
                    for c in range(8):
                        i = e.matmul(ps[0:n, p, half * 512:(half + 1) * 512], xnT[:, c, 0:n],
                                     wg[g][:, c, half * 512:(half + 1) * 512], start=(c == 0), stop=(c == 7))
                return i
            yield P.add("pe", mm, r=["xnT", ("wg", g)], w=[("ps", p)])
            return p

        def rope(p, n, dst, dstkey, dst32=None):
            src16 = ps[0:n, p, :].rearrange("p (a d) -> p a d", d=64)
            src4 = ps[0:n, p, :].rearrange("p (h t d) -> p h t d", h=H, t=2)
            t4 = tmpB[0:n, :].rearrange("p (h t d) -> p h t d", h=H, t=2)

            def t2(e):
                e.tensor_tensor(out=t4[:, :, 0, :], in0=src4[:, :, 1, :],
                                in1=rtab[0:n, 2, :].unsqueeze(1).to_broadcast([n, H, 64]), op=ALU.mult)
                return e.tensor_tensor(out=t4[:, :, 1, :], in0=src4[:, :, 0, :],
                                       in1=rtab[0:n, 1, :].unsqueeze(1).to_broadcast([n, H, 64]), op=ALU.mult)
            yield P.add("dve", t2, r=[("ps", p), "rtab"], w=["tmpB"])
            yield P.add("dve", lambda e: e.tensor_tensor(
                out=src16, in0=src16, in1=rtab[0:n, 0, :].unsqueeze(1).to_broadcast([n, 16, 64]), op=ALU.mult),
                r=[("ps", p), "rtab"], w=[("ps", p)])
            yield P.add("dve", lambda e: e.tensor_tensor(out=dst[0:n, :], in0=ps[0:n, p, :], in1=tmpB[0:n, :], op=ALU.add),
                        r=[("ps", p), "tmpB"], w=[dstkey])
            if dst32 is not None:
                yield P.add("dve", lambda e: e.tensor_tensor(out=tmpB[0:32, :], in0=ps[0:32, p, :], in1=tmpB[0:32, :], op=ALU.add),
                            r=[("ps", p), "tmpB"], w=["tmpB"])
                px = yield from alloc()

                def tr32(e):
                    for h in range(H):
                        i = e.transpose(ps[:, px, h * 32:(h + 1) * 32], tmpB[0:32, h * 128:(h + 1) * 128], identf[0:32, 0:32])
                    return i
                yield P.add("pe", tr32, r=["tmpB", "identf"], w=[("ps", px)])
                yield P.add("act", lambda e: e.activation(out=dst32, in_=ps[:, px, 0:256].rearrange("p (h t) -> p h t", h=H),
                                                          func=AF.Copy), r=[("ps", px)], w=vb2keys)
                free(px)
            free(p, late=True)

        def transpose_heads(src, srckey, n, dst, dstkey):
            p = yield from alloc()
            pb = PSb(p)

            def tr(e):
                for h in range(H):
                    i = e.transpose(pb[:, h * 128:h * 128 + n], src[0:n, h * 128:(h + 1) * 128], identb[0:n, 0:n])
                return i
            yield P.add("pe", tr, r=[srckey, "identb"], w=[("ps", p)])
            yield P.add("act", lambda e: e.activation(
                out=dst[:, :, 0:n], in_=pb[:, 0:1024].rearrange("p (h t) -> p h t", h=H)[:, :, 0:n], func=AF.Copy),
                r=[("ps", p)], w=[dstkey])
            free(p)

        def layernorm_vg(p, n, sample, l):
            def stats(e):
                e.bn_stats(bst[0:n, 0:6], ps[0:n, p, 0:512])
                return e.bn_stats(bst[0:n, 6:12], ps[0:n, p, 512:1024])
            yield P.add("dve", stats, r=[("ps", p)], w=["bst"])
            yield P.add("dve", lambda e: e.bn_aggr(mv[0:n, :], bst[0:n, :]), r=["bst"], w=["mv"])
            yield P.add("dve", lambda e: e.tensor_scalar(out=lsd[0:n, :], in0=mv[0:n, 1:2], scalar1=EPS, scalar2=None,
                                                         op0=ALU.add), r=["mv"], w=["lsd"])
            yield P.add("pool", lambda e: e.tensor_tensor(out=lsd[0:n, :], in0=lsd[0:n, :], in1=mhalf[0:n, :], op=ALU.pow),
                        r=["lsd", "mhalf"], w=["lsd"])
            yield P.add("dve", lambda e: e.tensor_scalar(out=nmr[0:n, :], in0=mv[0:n, 0:1], scalar1=lsd[0:n, 0:1], scalar2=-1.0,
                                                         op0=ALU.mult, op1=ALU.mult), r=["mv", "lsd"], w=["nmr"])
            yield P.add("act", lambda e: e.activation(out=tmpB[0:n, :], in_=ps[0:n, p, :], func=AF.Identity,
                                                      bias=nmr[0:n, :], scale=lsd[0:n, :]),
                        r=[("ps", p), "nmr", "lsd"], w=["tmpB"])
            free(p)
            yield P.add("pool", lambda e: e.tensor_tensor(out=tmpB[0:n, :], in0=tmpB[0:n, :], in1=lng[0:n, :], op=ALU.mult),
                        r=["tmpB", "lng"], w=["tmpB"])
            if sample:
                yield P.add("pool", lambda e: e.tensor_tensor(out=tmpB[0:n, :], in0=tmpB[0:n, :], in1=lnb[0:n, :], op=ALU.add),
                            r=["tmpB", "lnb"], w=["tmpB"])
                yield P.add("sp", lambda e: e.dma_start(out=gv_out[l], in_=tmpB[0:n, :]), r=["tmpB"], w=[("gv", l)],
                            dma=True, semkey="tmpB")
            else:
                yield P.add("pool", lambda e: e.tensor_tensor(out=vn[0:n, :], in0=tmpB[0:n, :], in1=lnb[0:n, :], op=ALU.add),
                            r=["tmpB", "lnb"], w=["vn"])

        def groupnorm_gate(po, n, sgt, sgkey, src=None, srckey=None):
            if src is None:
                src = lambda h: ps[0:n, po, h * 128:(h + 1) * 128]
                srckey = ("ps", po)

            def sq(e):
                for h in range(H):
                    i = e.activation(out=junk[0:n, :], in_=src(h), func=AF.Square,
                                     accum_out=oss[0:n, h:h + 1])
                return i
            yield P.add("act", sq, r=[srckey], w=["oss", ("osb", 0)])
            yield P.add("dve", lambda e: e.tensor_scalar(out=oss[0:n, :], in0=oss[0:n, :], scalar1=1.0 / 128, scalar2=EPS,
                                                         op0=ALU.mult, op1=ALU.add), r=["oss"], w=["oss"])
            yield P.add("pool", lambda e: e.tensor_tensor(out=orstd[0:n, :], in0=oss[0:n, :],
                                                          in1=mhalf[0:n, :].to_broadcast([n, H]), op=ALU.pow),
                        r=["oss", "mhalf"], w=["orstd"])

            def gate(e):
                for h in range(H):
                    i = e.scalar_tensor_tensor(out=cat[0:n, h * 128:(h + 1) * 128], in0=src(h),
                                               scalar=orstd[0:n, h:h + 1], in1=sgt[0:n, h * 128:(h + 1) * 128],
                                               op0=ALU.mult, op1=ALU.mult)
                return i
            yield P.add("dve", gate, r=[srckey, "orstd", sgkey], w=["cat_o"])
            if po is not None:
                free(po, late=True)

        def out_proj_residual(n, hb, hkey, seg=False):
            p = yield from alloc()
            pb = PSb(p)

            def tr(e):
                for c in range(16):
                    i = e.transpose(pb[:, c * 128:c * 128 + n], cat[0:n, c * 128:(c + 1) * 128], identb[0:n, 0:n])
                return i
            yield P.add("pe", tr, r=["cat_o", "cat_m", "identb"], w=[("ps", p)])
            yield P.add("act", lambda e: e.activation(
                out=catT[:, :, 0:n], in_=pb[:, :].rearrange("p (c t) -> p c t", c=16)[:, :, 0:n], func=AF.Copy),
                r=[("ps", p)], w=["catT"])
            free(p)
            if seg:
                yield "seg"
            py = yield from alloc()

            def mm(e):
                for half in range(2):
                    for c in range(16):
                        i = e.matmul(ps[0:n, py, half * 512:(half + 1) * 512], catT[:, c, 0:n],
                                     wo[c // 8][:, c % 8, half * 512:(half + 1) * 512], start=(c == 0), stop=(c == 15))
                return i
            yield P.add("pe", mm, r=["catT", ("wo", 0), ("wo", 1)], w=[("ps", py)])
            yield P.add("dve", lambda e: e.tensor_tensor(out=hb[0:n, :], in0=ps[0:n, py, :], in1=hb[0:n, :], op=ALU.add),
                        r=[("ps", py), hkey], w=[hkey])
            free(py)

        def final_norm_store(n, hb, hkey, dst, dstkey):
            yield P.add("act", lambda e: e.activation(out=junk2[0:n, :], in_=hb[0:n, :], func=AF.Square, accum_out=ssq2[0:n, :]),
                        r=[hkey], w=["catT", "ssq2"])
            yield P.add("dve", lambda e: e.tensor_scalar(out=ssq2[0:n, :], in0=ssq2[0:n, :], scalar1=1.0 / D, scalar2=EPS,
                                                         op0=ALU.mult, op1=ALU.add), r=["ssq2"], w=["ssq2"])
            yield P.add("pool", lambda e: e.tensor_tensor(out=rstd2[0:n, :], in0=ssq2[0:n, :], in1=mhalf[0:n, :], op=ALU.pow),
                        r=["ssq2", "mhalf"], w=["rstd2"])
            yield P.add("dve", lambda e: e.scalar_tensor_tensor(out=hb[0:n, :], in0=hb[0:n, :], scalar=rstd2[0:n, 0:1],
                                                                in1=fing[0:n, :], op0=ALU.mult, op1=ALU.mult),
                        r=[hkey, "rstd2", "fing"], w=[hkey])
            yield P.add("pool", lambda e: e.dma_start(out=dst, in_=hb[0:n, :]), r=[hkey], w=[dstkey], dma=True, semkey=hkey)

        ver = {}

        def setv(key, c, sample):
            ver[key] = ("s" if sample else c)

        def chk(keys, c):
            for k in keys:
                assert ver.get(k) == c, ("stale/early buffer", k, ver.get(k), c)

        def bufs(c, sample):
            par = 0 if sample else (c + 1) % 3
            return (hbuf[par], ("hbuf", par), vb[0], ("vb", 0), sg[0], ("sg", 0))

        def stageA(l, c, sample, part=0):
            n = NS if sample else 128
            hb, hkey, vbt, vkey, sgt, sgkey = bufs(c, sample)
            if l == 0:
                src = xs_in if sample else xp[c * 128:(c + 1) * 128, :]
            else:
                src = h1[SEQ:SEQ + NS, :] if sample else h1[c * 128:(c + 1) * 128, :]
            srckey = ("h1", "s" if sample else c)

            def seg_norm(cc=c):
                hb2, hkey2 = bufs(cc, sample)[0:2]
                if l == 0:
                    src2 = xs_in if sample else xp[cc * 128:(cc + 1) * 128, :]
                else:
                    src2 = h1[SEQ:SEQ + NS, :] if sample else h1[cc * 128:(cc + 1) * 128, :]
                srckey2 = ("h1", "s" if sample else cc)
                if not sample:
                    yield P.add("sp", lambda e: e.dma_start(out=hb2[0:n, :], in_=src2), r=[srckey2] if l else [], w=[hkey2],
                                dma=True, semkey=hkey2)
                    rsrc = c_ropeS if sample else c_ropeP[cc]
                    yield P.add("sp", lambda e: e.dma_start(out=rtab[0:n, :, :], in_=rsrc), w=["rtab"], dma=True, semkey="rtab")
                yield from rmsnorm_to_xnT(n, hb2, hkey2, split=(part == 0 and cc != c))

            def seg_q():
                pq = yield from proj(0, n)
                if not sample:
                    def presq(e):
                        for h in range(H):
                            i = e.activation(out=ps[0:n, pq, h * 128:(h + 1) * 128], in_=ps[0:n, pq, h * 128:(h + 1) * 128],
                                             func=AF.Copy, scale=qdec[0:n, h:h + 1])
                        return i
                    yield P.add("act", presq, r=[("ps", pq), "qdec"], w=[("ps", pq)])
                if part == 0:
                    yield "seg"
                yield from rope(pq, n, q_rot, "q_rot", dst32=(qT32 if (not sample and c == 0) else None))
                setv("q_rot", c, sample)

            def seg_k():
                pk = yield from proj(1, n)
                if sample:
                    yield P.add("act", lambda e: e.activation(out=ps[0:n, pk, :], in_=ps[0:n, pk, :], func=AF.Copy, scale=128.0 ** -0.5),
                                r=[("ps", pk)], w=[("ps", pk)])
                else:
                    def presk(e):
                        for h in range(H):
                            i = e.activation(out=ps[0:n, pk, h * 128:(h + 1) * 128], in_=ps[0:n, pk, h * 128:(h + 1) * 128],
                                             func=AF.Copy, scale=kinv[0:n, h:h + 1])
                        return i
                    yield P.add("act", presk, r=[("ps", pk), "kinv"], w=[("ps", pk)])
                if part == 0:
                    yield "seg"
                yield from rope(pk, n, k_rot, "k_rot", dst32=(kT32 if (not sample and c == 0) else None))
                setv("k_rot", c, sample)

            def seg_v():
                pv = yield from proj(2, n)
                yield P.add("act", lambda e: e.activation(out=vbt[0:n, :], in_=ps[0:n, pv, :], func=AF.Copy), r=[("ps", pv)], w=[vkey])
                free(pv)
                setv(vkey, c, sample)

            def seg_gr():
                pg = yield from proj(3, n)
                yield P.add("act", lambda e: e.activation(out=sgt[0:n, :], in_=ps[0:n, pg, :], func=AF.Silu), r=[("ps", pg)], w=[sgkey])
                free(pg)
                setv(sgkey, c, sample)

            def seg_gg():
                pgg = yield from proj(6, n)
                yield P.add("act", lambda e: e.activation(out=sgg[0:n, :], in_=ps[0:n, pgg, :], func=AF.Silu), r=[("ps", pgg)], w=["sgg"])
                free(pgg)

            def seg_u():
                pu = yield from proj(4, n)
                yield P.add("dve", lambda e: e.tensor_tensor(out=sgg[0:n, :], in0=ps[0:n, pu, :], in1=sgg[0:n, :], op=ALU.mult),
                            r=[("ps", pu), "sgg"], w=["sgg"])
                free(pu)
                setv("sgg", c, sample)

            def seg_vg():
                pvg = yield from proj(5, n)
                yield from layernorm_vg(pvg, n, sample, l)
                setv("vn", c, sample)

            if part == 0:
                segs = [seg_q, seg_k, seg_v, seg_vg, seg_gr, seg_gg]
                if c + 1 < NCH:
                    nrm = seg_norm(c + 1)

                    def nrm_a():
                        for r in nrm:
                            if r == "seg":
                                return
                            yield r

                    def nrm_b():
                        yield from nrm
                    segs += [nrm_a, seg_u, nrm_b]
                else:
                    segs.append(seg_u)
            elif part == 3:
                segs = [seg_norm]
            elif part == 1:
                segs = [seg_norm, seg_q, seg_k, seg_v]
            else:
                segs = [seg_gr, seg_gg, seg_u, seg_vg]
            for i, sgm in enumerate(segs):
                yield from sgm()
                if i + 1 < len(segs):
                    yield "seg"

        def stageB(l, c):
            n = 128
            hb, hkey, vbt, vkey, sgt, sgkey = bufs(c, False)
            ptq = yield from alloc()
            pbq = PSb(ptq)

            def trqk(e):
                for h in range(H):
                    i = e.transpose(pbq[:, h * 128:(h + 1) * 128], q_rot[:, h * 128:(h + 1) * 128], identb[:])
                for h in range(H):
                    i = e.transpose(pbq[:, D + h * 128:D + (h + 1) * 128], k_rot[:, h * 128:(h + 1) * 128], identb[:])
                return i
            chk(["q_rot", "k_rot"], c)
            yield P.add("pe", trqk, r=["q_rot", "k_rot", "identb"], w=[("ps", ptq)])
            yield P.add("act", lambda e: e.activation(out=cat[:, :], in_=pbq[:, :], func=AF.Copy),
                        r=[("ps", ptq)], w=["cat_o", "cat_m"])
            free(ptq)
            pkv = yield from alloc()

            def mm_kv(e):
                for h in range(H):
                    i = e.matmul(ps[:, pkv, h * 128:(h + 1) * 128], k_rot[:, h * 128:(h + 1) * 128],
                                 vbt[:, h * 128:(h + 1) * 128], start=True, stop=True)
                return i
            chk(["k_rot", vkey], c)
            yield P.add("pe", mm_kv, r=["k_rot", vkey], w=[("ps", pkv)])
            if c == 0:
                yield P.add("dve", lambda e: e.tensor_copy(S32[:], ps[:, pkv, :]), r=[("ps", pkv)], w=["S32"])
            else:
                def upd(e):
                    for h in range(H):
                        i = e.scalar_tensor_tensor(out=S32[:, h * 128:(h + 1) * 128], in0=S32[:, h * 128:(h + 1) * 128],
                                                   scalar=_C["sdec"][h], in1=ps[:, pkv, h * 128:(h + 1) * 128],
                                                   op0=ALU.mult, op1=ALU.add)
                    return i
                yield P.add("dve", upd, r=[("ps", pkv), "S32"], w=["S32"])
            free(pkv)
            yield "seg"
            psc = yield from alloc()

            def mm_sc(e):
                for h in range(H):
                    i = e.matmul(ps[:, psc, h * 128:(h + 1) * 128], kkT[:, h, :], qdT[:, h, :], start=True, stop=True)
                    if c == 0:
                        i = e.matmul(ps[0:32, psc, h * 128:h * 128 + 32], kT32[:, h, :], qT32[:, h, :], start=True, stop=True)
                return i
            yield P.add("pe", mm_sc, r=["cat_m", "cat_o"] + (vb2keys if c == 0 else []), w=[("ps", psc)])
            yield P.add("dve", lambda e: e.tensor_tensor(
                out=scT, in0=ps[:, psc, :].rearrange("p (h i) -> p h i", h=H),
                in1=maskT[:].unsqueeze(1).to_broadcast([128, H, 128]), op=ALU.mult),
                r=[("ps", psc), "maskT"], w=["catT"])
            free(psc)
            yield "seg"
            po = yield from alloc()

            def mm_o(e):
                for h in range(H):
                    i = e.matmul(ps[:, po, h * 128:(h + 1) * 128], scT[:, h, :], vbt[:, h * 128:(h + 1) * 128],
                                 start=True, stop=(c == 0))
                    if c > 0:
                        i = e.matmul(ps[:, po, h * 128:(h + 1) * 128], qdT[:, h, :], Sbf[:, h, :], start=False, stop=True)
                return i
            chk([vkey], c)
            yield P.add("pe", mm_o, r=["catT", vkey, "cat_o"] + (["Sbf"] if c > 0 else []), w=[("ps", po)])
            yield from groupnorm_gate(po, n, sgt, sgkey)
            yield "seg"
            pgm = yield from alloc()

            def mm_g(e):
                for h in range(H):
                    i = e.matmul(ps[:, pgm, h * 128:(h + 1) * 128], WT[:, h, :], vn[:, h * 128:(h + 1) * 128],
                                 start=True, stop=True)
                return i
            chk(["vn", "sgg", sgkey], c)
            yield P.add("pe", mm_g, r=["WT", "vn"], w=[("ps", pgm)])

            def gm(e):
                for h in range(H):
                    i = e.scalar_tensor_tensor(out=cat[0:n, D + h * 128:D + (h + 1) * 128],
                                               in0=ps[0:n, pgm, h * 128:(h + 1) * 128], scalar=bsT[0:n, h:h + 1],
                                               in1=sgg[0:n, h * 128:(h + 1) * 128], op0=ALU.add, op1=ALU.mult)
                return i
            yield P.add("dve", gm, r=[("ps", pgm), "bsT", "sgg"], w=["cat_m"])
            free(pgm, late=True)
            yield "seg"
            yield from out_proj_residual(n, hb, hkey, seg=True)
            if l == 0:
                dst = h1[c * 128:(c + 1) * 128, :]
                yield P.add("pool", lambda e: e.dma_start(out=dst, in_=hb[0:n, :]), r=[hkey], w=[("h1", c)], dma=True, semkey=hkey)
            else:
                yield from final_norm_store(n, hb, hkey, yp[c * 128:(c + 1) * 128, :], ("yp", c))
            if c < NCH - 1:
                def sbf(e):
                    for h in range(H):
                        i = e.activation(out=Sbf[:, h, :], in_=S32[:, h * 128:(h + 1) * 128], func=AF.Copy,
                                         scale=_C["sdec"][h])
                    return i
                yield P.add("act", sbf, r=["S32"], w=["Sbf"])
            else:
                def sfin(e):
                    for h in range(H):
                        i = e.activation(out=S32[:, h * 128:(h + 1) * 128], in_=S32[:, h * 128:(h + 1) * 128],
                                         func=AF.Copy, scale=_C["sdec"][h])
                    return i
                yield P.add("act", sfin, r=["S32"], w=["S32"])
                yield P.add("sp", lambda e: e.dma_start(out=sp_out[l].rearrange("h d v -> d h v"),
                                                        in_=S32[:].rearrange("p (h v) -> p h v", h=H)),
                            r=["S32"], w=[("spo", l)], dma=True, semkey="S32")

        def sample_relayout(l):
            n = NS
            hb, hkey, vbt, vkey, sgt, sgkey = bufs(0, True)
            yield P.add("sp", lambda e: e.dma_start(out=qkvscr[2], in_=vbt[0:n, :]), r=[vkey], w=[("scr", 2)],
                        dma=True, semkey=vkey)
            for t, key, dst, dkey in [(q_rot, "q_rot", q2all, "q2all"), (k_rot, "k_rot", k2all, "k2all")]:
                p = yield from alloc()
                pb = PSb(p)
                tv8 = t[0:n, :].rearrange("b (p dl) -> b dl p", dl=8)

                def tr(e, pb=pb, tv8=tv8):
                    for dl in range(8):
                        i = e.transpose(pb[:, dl * NS:(dl + 1) * NS], tv8[:, dl, :], identb[0:n, 0:n])
                    return i
                yield P.add("pe", tr, r=[key, "identb"], w=[("ps", p)])
                yield P.add("act", (lambda pb, dst: lambda e: e.activation(
                    out=dst[:].rearrange("p b dl -> p dl b"), in_=pb[:, 0:8 * NS].rearrange("p (dl b) -> p dl b", dl=8),
                    func=AF.Copy))(pb, dst), r=[("ps", p)], w=[dkey])
                free(p)

        def sample_loop(l):
            n = NS
            hb, hkey, vbt, vkey, sgt, sgkey = bufs(0, True)
            stbuf = [(S32, "S32"), (hbuf[1], ("hbuf", 1)), (hbuf[2], ("hbuf", 2)), (tmpB, "tmpB")]

            def front(b):
                Sb, Skey = stbuf[b % 4]
                pre = (l, b) in preloaded
                if b % 4 == 0:
                    g0 = (b // 4) % 2
                    for h in range(H):
                        yield P.add("sp", (lambda h: lambda e: e.dma_start(
                            out=vb2[h * 16:(h + 1) * 16, g0 * 4:(g0 + 1) * 4, :],
                            in_=qkvscr[2, b:b + 4, h * 128:(h + 1) * 128].partition_broadcast(16)))(h),
                            r=[("scr", 2)], w=[("vb2", h, g0)], dma=True, semkey=("vb2", h, g0))
                if not pre:
                    yield P.add("sp", lambda e: e.dma_start(
                        out=Sb[:], in_=st_in[l, b].rearrange("h (dh dl) v -> (h dh) (dl v)", dl=8)),
                        w=[Skey], dma=True, semkey=Skey)
                yield P.add("act", lambda e: e.activation(out=Sb[:], in_=Sb[:], func=AF.Copy, scale=gamP[:, 0:1]),
                            r=[Skey, "gamP"], w=[Skey])

                def upd(e):
                    for dl in range(8):
                        i = e.scalar_tensor_tensor(out=Sb[:, dl * 128:(dl + 1) * 128], in0=vb2[:, b % 8, :],
                                                   scalar=k2all[:, b, dl:dl + 1], in1=Sb[:, dl * 128:(dl + 1) * 128],
                                                   op0=ALU.mult, op1=ALU.add)
                    return i
                yield P.add("dve", upd, r=[Skey, "k2all"] + [("vb2", h, (b // 4) % 2) for h in range(H)], w=[Skey])
                qm = Q2[b % 2]
                yield P.add("dve", lambda e: e.tensor_tensor(
                    out=qm[:], in0=q2all[:, b, :].unsqueeze(2).to_broadcast([128, 8, H]),
                    in1=blockind[:].unsqueeze(1).to_broadcast([128, 8, H]), op=ALU.mult),
                    r=["q2all", "blockind"], w=[("Q2", b % 2)])

            pobs = {}

            def back(b):
                Sb, Skey = stbuf[b % 4]
                qm = Q2[b % 2]
                qmkey = ("Q2", b % 2)
                Sbv = Sbf[:].rearrange("p h v -> p (h v)")
                yield P.add("act", lambda e: e.activation(out=Sbv, in_=Sb[:], func=AF.Copy), r=[Skey], w=["Sbf"])
                yield P.add("act", lambda e: e.dma_start(
                    out=ss_out[l, b].rearrange("h (dh dl) v -> (h dh) (dl v)", dl=8), in_=Sb[:]),
                    r=[Skey], w=[("sso", l, b)], dma=True, semkey=Skey)
                pob = yield from alloc()

                def mm_os(e):
                    for dl in range(8):
                        i = e.matmul(ps[0:H, pob, 0:128], qm[:, dl, :], Sbv[:, dl * 128:(dl + 1) * 128],
                                     start=(dl == 0), stop=(dl == 7))
                    return i
                yield P.add("pe", mm_os, r=[qmkey, "Sbf"], w=[("ps", pob)])
                pobs[b] = pob

            def tail(b):
                pob = pobs.pop(b)
                ob = osb[0]
                obkey = ("osb", 0)
                yield P.add("act", lambda e: e.activation(out=ob, in_=ps[0:H, pob, 0:128], func=AF.Copy),
                            r=[("ps", pob)], w=[obkey])
                free(pob)
                yield P.add("act", lambda e: e.dma_start(out=oscr[b].rearrange("(h v) -> h v", h=H), in_=ob),
                            r=[obkey], w=[("oscr", b)], dma=True, semkey=obkey)

            yield from front(0)
            yield from front(1)
            for b in range(NS):
                yield from back(b)
                if b >= 1:
                    yield from tail(b - 1)
                if b + 2 < NS:
                    yield from front(b + 2)
                if deferred_w and b % 2 == 1:
                    load_weight_group(*deferred_w.pop(0))
            yield from tail(NS - 1)
            while deferred_w:
                load_weight_group(*deferred_w.pop(0))
            yield P.add("sp", lambda e: e.dma_start(out=hbuf[2][0:n, :], in_=oscr),
                        r=[("oscr", b) for b in range(NS)], w=[("hbuf", 2)], dma=True, semkey=("hbuf", 2))

        def sample_stream(l):
            n = NS
            hb, hkey, vbt, vkey, sgt, sgkey = bufs(0, True)
            yield from stageA(l, 0, True, part=1)
            yield from sample_relayout(l)
            if l != 0:
                yield from wt_prep(l, catf, ["cat_o", "cat_m"])
            if l == 0:
                yield from sample_loop(l)
            yield from stageA(l, 0, True, part=2)
            tv = tmpB[0:n, :].rearrange("p (h g) -> p h g", h=H)
            yield P.add("dve", lambda e: e.tensor_tensor(out=tv, in0=tv, in1=w00[:].unsqueeze(2).to_broadcast([n, H, 128]),
                                                         op=ALU.mult), r=["tmpB", "w00"], w=["tmpB"])
            yield P.add("dve", lambda e: e.tensor_tensor(out=tv, in0=tv, in1=b00[:].unsqueeze(2).to_broadcast([n, H, 128]),
                                                         op=ALU.add), r=["tmpB", "b00"], w=["tmpB"])
            yield P.add("dve", lambda e: e.tensor_tensor(out=cat[0:n, D:2 * D], in0=tmpB[0:n, :], in1=sgg[0:n, :], op=ALU.mult),
                        r=["tmpB", "sgg"], w=["cat_m"])
            if l != 0:
                yield from sample_loop(l)

        def sample_tail(l):
            n = NS
            hb, hkey, vbt, vkey, sgt, sgkey = bufs(0, True)
            yield from groupnorm_gate(None, n, sgt, sgkey, src=lambda h: hbuf[2][0:n, h * 128:(h + 1) * 128],
                                      srckey=("hbuf", 2))
            yield "seg"
            yield from out_proj_residual(n, hb, hkey, seg=True)
            if l == 0:
                yield P.add("sp", lambda e: e.dma_start(out=h1[SEQ:SEQ + NS, :], in_=hb[0:n, :]), r=[hkey], w=[("h1", "s")],
                            dma=True, semkey=hkey)
            else:
                yield from final_norm_store(n, hb, hkey, ys, ("ys",))

        def drive(streams):
            prog = [0.0] * len(streams)
            alive = [True] * len(streams)
            while any(alive):
                order = sorted([i for i in range(len(streams)) if alive[i]], key=lambda i: prog[i])
                stepped = False
                for i in order:
                    try:
                        r = next(streams[i][0])
                    except StopIteration:
                        alive[i] = False
                        stepped = True
                        break
                    if r == "blocked":
                        continue
                    if r == "seg":
                        stepped = True
                        break
                    prog[i] += 1.0 / streams[i][1]
                    stepped = True
                    break
                assert stepped, "all streams blocked on PSUM allocation"

        def drive_script(ga, gb, pattern):
            gens = {"A": ga, "B": gb}
            done = {"A": False, "B": False}
            for who in pattern:
                if done[who]:
                    continue
                while True:
                    try:
                        r = next(gens[who])
                    except StopIteration:
                        done[who] = True
                        break
                    assert r != "blocked", "PSUM alloc blocked in scripted merge"
                    if r == "seg":
                        break
            for who in ("A", "B"):
                if not done[who]:
                    for r in gens[who]:
                        assert r != "blocked"

        preloaded = set()

        def preload_states(l):
                for b, (Sb, Skey) in enumerate([(S32, "S32"), (hbuf[1], ("hbuf", 1)), (hbuf[2], ("hbuf", 2))]):
                    P.add("sp", (lambda Sb, b, l: lambda e: e.dma_start(
                        out=Sb[:], in_=st_in[l, b].rearrange("h (dh dl) v -> (h dh) (dl v)", dl=8)))(Sb, b, l),
                        w=[Skey], dma=True, semkey=Skey)
                    preloaded.add((l, b))

        for l in range(DEPTH):
            if l == 0:
                sample_input_loads(l)
                preload_states(l)
                small_loads(l, fing, "fing")
                load_weights(l)
            else:
                small_loads(l, None, None)
                preload_states(l)
            drive([[sample_stream(l), 1]])
            if l == 0:
                drive([[wt_prep(l, fing, "fing"), 1]])
                ld(fing[:], fin_g.partition_broadcast(128), "fing")

            def first_A(l=l):
                yield from stageA(l, 0, False, part=3)
                yield "seg"
                yield from stageA(l, 0, False)
            drive_script(first_A(), sample_tail(l), ["B", "A", "A", "A", "B", "A", "A", "B"] + ["A"] * 12)
            for c in range(NCH):
                if l == 0 and c == 2:
                    precast_weights(1)
                if c + 1 < NCH:
                    drive_script(stageA(l, c + 1, False), stageB(l, c), PATTERN)
                else:
                    if l + 1 < DEPTH:
                        sample_input_loads(l + 1)
                        load_weights_bf(l + 1, "in")
                    drive([[stageB(l, c), 1]])
                    if l + 1 < DEPTH:
                        load_weights_bf(l + 1, "out")
                        stage_ws(l + 1, catf, ["cat_o", "cat_m"])
        outkeys = [("ys",)] + [("yp", c) for c in range(NCH)] + [("spo", l) for l in range(DEPTH)] + \
                  [("sso", l, b) for l in range(DEPTH) for b in range(NS)] + [("gv", l) for l in range(DEPTH)]
        P.add("sp", lambda e: e.nop(), r=outkeys)
        P.emit(st)
        nc._prog = P
    return nc


PATTERN = ["B", "A", "B", "A", "A", "B", "A", "A", "B", "A", "A", "B", "A", "A", "B", "A", "A"]

_NC_CACHE = {}


def kernel(x_prompt, x_sample, state_ret, norm_g, w_in, w_out, gm_ws, gm_b, gm_ln_g, gm_ln_b, final_g):
    f = lambda a: np.ascontiguousarray(np.asarray(a, dtype=np.float32))
    x_prompt, x_sample, state_ret = f(x_prompt), f(x_sample), f(state_ret)
    shared = dict(norm_g=f(norm_g), w_in=f(w_in), w_out=f(w_out), gm_ws=f(gm_ws), gm_b=f(gm_b), ln_g=f(gm_ln_g),
                  ln_b=f(gm_ln_b), fin_g=f(final_g), c_qdec=_C["qdec"], c_kinv=_C["kinv"], c_ropeP=_C["ropeP"],
                  c_ropeS=_C["ropeS"], c_maskT=_C["maskT"], c_ident=_C["ident"], c_gamP=_C["gamP"], c_blockind=_C["blockind"])
    in_maps = []
    for c in range(NCORES):
        m = dict(shared)
        m["xp"] = x_prompt[c]
        m["xs"] = np.ascontiguousarray(x_sample[c * NS:(c + 1) * NS, 0, :])
        m["st"] = np.ascontiguousarray(state_ret[:, c * NS:(c + 1) * NS])
        in_maps.append(m)
    if "nc" not in _NC_CACHE:
        _NC_CACHE["nc"] = build_nc()
    res = run_bass_kernel_spmd(_NC_CACHE["nc"], in_maps, core_ids=list(range(NCORES)))
    R = res.results
    y_prompt = np.stack([R[c]["yp"] for c in range(NCORES)], 0)
    y_sample = np.concatenate([R[c]["ys"] for c in range(NCORES)], 0)[:, None, :]
    sp = np.stack([R[c]["sp_out"] for c in range(NCORES)], 1)
    ss = np.concatenate([R[c]["ss_out"] for c in range(NCORES)], 1)
    gv = np.concatenate([R[c]["gv_out"] for c in range(NCORES)], 1).reshape(DEPTH, NCORES * NS, 1, H, 128)
    return (y_prompt.astype(np.float32), y_sample.astype(np.float32), sp.astype(np.float32),
            ss.astype(np.float32), gv.astype(np.float32))
```

```python
import numpy as np
from contextlib import ExitStack
import concourse.bass as bass
import concourse.mybir as mybir
from concourse.bass_utils import run_bass_kernel_spmd

F32 = mybir.dt.float32
BF16 = mybir.dt.bfloat16
AF = mybir.ActivationFunctionType
ALU = mybir.AluOpType

D = 1024
SEQ = 2048
NCH = 16
DEPTH = 2
NS = 16
H = 8
PAST = 16384
EPS = 1e-6
NCORES = 8


class _Op:
    __slots__ = ("eng", "fn", "deps", "dma", "semkey", "sig", "waits", "r", "w")

    def __init__(self, eng, fn, deps, dma, semkey):
        self.eng, self.fn, self.deps, self.dma, self.semkey = eng, fn, deps, dma, semkey
        self.sig = None
        self.waits = []


class Prog:
    def __init__(self, nc):
        self.nc = nc
        self.ops = []
        self.last_w = {}
        self.readers = {}

    def add(self, eng, fn, r=(), w=(), dma=False, semkey=None):
        i = len(self.ops)
        deps = {}
        for k in r:
            j = self.last_w.get(k)
            if j is not None:
                deps[j] = True
        for k in w:
            j = self.last_w.get(k)
            if j is not None:
                deps.setdefault(j, False)
            for j in self.readers.get(k, ()):
                deps.setdefault(j, False)
        for k in r:
            self.readers.setdefault(k, []).append(i)
        for k in w:
            self.last_w[k] = i
            self.readers[k] = []
        if dma:
            assert semkey is not None
        op = _Op(eng, fn, deps, dma, semkey)
        op.r, op.w = list(r), list(w)
        self.ops.append(op)
        return i

    def emit(self, stack):
        nc = self.nc
        ops = self.ops
        for i, op in enumerate(ops):
            for j, raw in op.deps.items():
                p = ops[j]
                if p.dma:
                    need = True
                elif p.eng == op.eng:
                    if op.dma:
                        need = True
                    elif p.eng == "pe":
                        need = False
                    else:
                        need = True
                else:
                    need = True
                if need:
                    op.waits.append(j)
                    p.sig = True
        cnt = {}
        sems = {}
        for op in ops:
            if not op.sig:
                continue
            k = ("d", op.semkey) if op.dma else ("e", op.eng)
            cnt[k] = cnt.get(k, 0) + (16 if op.dma else 1)
            op.sig = (k, cnt[k])
            sems[k] = None
        for n, k in enumerate(sems):
            sems[k] = stack.enter_context(nc.semaphore("sem%d" % n))
        self.nsems = len(sems)
        block = stack.enter_context(nc.Block())

        def replay(engname, e):
            waited = {}
            for op in ops:
                if op.eng != engname:
                    continue
                need = {}
                for j in op.waits:
                    k, v = ops[j].sig
                    if waited.get(k, 0) >= v:
                        continue
                    need[k] = max(need.get(k, 0), v)
                for k, v in need.items():
                    e.wait_ge(sems[k], v)
                    waited[k] = v
                inst = op.fn(e)
                if op.sig:
                    inst.then_inc(sems[op.sig[0]], 16 if op.dma else 1)

        @block.sync
        def _(e):
            replay("sp", e)

        @block.tensor
        def _(e):
            replay("pe", e)

        @block.scalar
        def _(e):
            replay("act", e)

        @block.vector
        def _(e):
            replay("dve", e)

        @block.gpsimd
        def _(e):
            replay("pool", e)


def _consts():
    lg = np.log(1.0 - 2.0 ** (-5.0 - np.arange(H, dtype=np.float64)))
    idx = np.arange(128, dtype=np.float64)
    qdec = np.exp(lg[None, :] * (idx[:, None] + 1.0))
    kinv = np.exp(-lg[None, :] * (idx[:, None] + 1.0)) * (128.0 ** -0.5)
    sdec = np.exp(lg * 128.0)
    gam = np.exp(lg)
    inv = (10000.0 ** (-np.arange(0, 128, 2, dtype=np.float32) / np.float32(128))).astype(np.float32)
    pos = np.arange(SEQ, dtype=np.float32)
    ang = (pos[:, None] * inv[None, :]).astype(np.float32).astype(np.float64)
    ropeP = np.stack([np.cos(ang), np.sin(ang), -np.sin(ang)], axis=1)
    ropeP = ropeP.reshape(NCH, 128, 3, 64).astype(np.float32)
    angs = (np.float32(PAST) * inv).astype(np.float32).astype(np.float64)
    ropeS = np.stack([np.cos(angs), np.sin(angs), -np.sin(angs)], axis=0)[None].repeat(NS, 0).astype(np.float32)
    maskT = (idx[:, None] <= idx[None, :]).astype(np.float32)
    ident = np.eye(128, dtype=np.float32)
    gamP = np.repeat(gam, 16).astype(np.float32).reshape(128, 1)
    blockind = (np.arange(128)[:, None] // 16 == np.arange(H)[None, :]).astype(np.float32)
    return dict(qdec=qdec.astype(np.float32), kinv=kinv.astype(np.float32), sdec=[float(x) for x in sdec],
                gam=[float(x) for x in gam], ropeP=ropeP, ropeS=ropeS, maskT=maskT, ident=ident, gamP=gamP, blockind=blockind)


_C = _consts()


def build_nc():
    nc = bass.Bass("TRN2", target_bir_lowering=False)
    dt_in = lambda name, shape: nc.dram_tensor(name, shape, F32, kind="ExternalInput").ap()
    dt_out = lambda name, shape: nc.dram_tensor(name, shape, F32, kind="ExternalOutput").ap()
    xp = dt_in("xp", [SEQ, D])
    xs_in = dt_in("xs", [NS, D])
    st_in = dt_in("st", [DEPTH, NS, H, 128, 128])
    norm_g = dt_in("norm_g", [DEPTH, D])
    w_in = dt_in("w_in", [DEPTH, D, 7 * D])
    w_out = dt_in("w_out", [DEPTH, 2 * D, D])
    gm_ws = dt_in("gm_ws", [DEPTH, H, 128, 128])
    gm_b = dt_in("gm_b", [DEPTH, H, 128])
    ln_g = dt_in("ln_g", [DEPTH, D])
    ln_b = dt_in("ln_b", [DEPTH, D])
    fin_g = dt_in("fin_g", [D])
    c_qdec = dt_in("c_qdec", [128, H])
    c_kinv = dt_in("c_kinv", [128, H])
    c_ropeP = dt_in("c_ropeP", [NCH, 128, 3, 64])
    c_ropeS = dt_in("c_ropeS", [NS, 3, 64])
    c_maskT = dt_in("c_maskT", [128, 128])
    c_ident = dt_in("c_ident", [128, 128])
    c_gamP = dt_in("c_gamP", [128, 1])
    c_blockind = dt_in("c_blockind", [128, H])
    yp = dt_out("yp", [SEQ, D])
    ys = dt_out("ys", [NS, D])
    sp_out = dt_out("sp_out", [DEPTH, H, 128, 128])
    ss_out = dt_out("ss_out", [DEPTH, NS, H, 128, 128])
    gv_out = dt_out("gv_out", [DEPTH, NS, D])
    h1 = nc.dram_tensor("h1", [SEQ + NS, D], F32, kind="Internal").ap()
    qkvscr = nc.dram_tensor("qkvscr", [3, NS, D], BF16, kind="Internal").ap()
    oscr = nc.dram_tensor("oscr", [NS, D], F32, kind="Internal").ap()
    wbf_in = nc.dram_tensor("wbf_in", [D, 7 * D], BF16, kind="Internal").ap()
    wbf_out = nc.dram_tensor("wbf_out", [2 * D, D], BF16, kind="Internal").ap()

    with ExitStack() as st:
        sb = lambda name, shape, dt: st.enter_context(nc.sbuf_tensor(name, shape, dt))
        wg = [sb("wg%d" % g, [128, 8, 1024], BF16) for g in range(7)]
        wo = [sb("wo%d" % g, [128, 8, 1024], BF16) for g in range(2)]
        identf = sb("identf", [128, 128], F32)
        identb = sb("identb", [128, 128], BF16)
        maskT = sb("maskT", [128, 128], F32)
        qdec = sb("qdec", [128, H], F32)
        kinv = sb("kinv", [128, H], F32)
        gamP = sb("gamP", [128, 1], F32)
        blockind = sb("blockind", [128, H], F32)
        rtab = sb("rtab", [128, 3, 64], F32)
        WT = sb("WT", [128, H, 128], BF16)
        ngT = sb("ngT", [128, 8], F32)
        bsT = sb("bsT", [128, H], F32)
        w00 = sb("w00", [NS, H], F32)
        b00 = sb("b00", [NS, H], F32)
        lng = sb("lng", [128, D], F32)
        lnb = sb("lnb", [128, D], F32)
        fing = sb("fing", [128, D], F32)
        hbuf = [sb("hbuf%d" % i, [128, D], F32) for i in range(3)]
        tmpB = sb("tmpB", [128, D], F32)
        S32 = sb("S32", [128, D], F32)
        xnT = sb("xnT", [128, 8, 128], BF16)
        q_rot = sb("q_rot", [128, D], BF16)
        k_rot = sb("k_rot", [128, D], BF16)
        vb = [sb("vb%d" % i, [128, D], BF16) for i in range(1)]
        vn = sb("vn", [128, D], BF16)
        sg = [sb("sg%d" % i, [128, D], BF16) for i in range(1)]
        sgg = sb("sgg", [128, D], BF16)
        Sbf = sb("Sbf", [128, H, 128], BF16)
        cat = sb("cat", [128, 2 * D], BF16)
        catT = sb("catT", [128, 16, 128], BF16)
        k2all = sb("k2all", [128, NS, 8], BF16)
        q2all = sb("q2all", [128, NS, 8], BF16)
        vb2 = sb("vb2", [128, 8, 128], BF16)
        Q2 = [sb("Q2_%d" % i, [128, 8, H], BF16) for i in range(2)]
        osbt = sb("osbt", [128, 128], F32)
        osb = [osbt[0:H, :]]
        junk = osbt[:, 0:64].bitcast(BF16)
        bst = sb("bst", [128, 12], F32)
        mv = sb("mv", [128, 2], F32)
        oss = sb("oss", [128, H], F32)
        orstd = sb("orstd", [128, H], F32)
        st1 = sb("st1", [128, 8], F32)
        ssq, ssq2, rstd2, rstd, lsd, nmr, epst = [st1[:, i:i + 1] for i in range(7)]
        ps = st.enter_context(nc.psum_tensor("ps", [128, 4, 1024], F32))

        P = Prog(nc)
        from collections import deque
        freep = {0: (-4, 0), 1: (-3, 0), 2: (-2, 0), 3: (-1, 0)}
        atick = [0]

        def alloc():
            while not freep:
                yield "blocked"
            atick[0] += 1
            p = min(freep, key=lambda k: freep[k][0] + freep[k][1])
            del freep[p]
            return p

        def free(p, late=False):
            freep[p] = (atick[0], 2 if late else 0)

        vb2keys = [("vb2", h, g0) for h in range(H) for g0 in range(2)]
        vb2f = vb2[:, :, :].rearrange("p g v -> p (g v)").bitcast(F32)
        qT32 = vb2f[:, 0:256].rearrange("p (h t) -> p h t", h=H)
        kT32 = vb2f[:, 256:512].rearrange("p (h t) -> p h t", h=H)

        def PSb(p):
            return ps[:, p, :].bitcast(BF16)

        xs = tmpB[:, 0:512].bitcast(BF16)
        scT = catT[:, 0:8, :]
        qdT = cat[:, 0:D].rearrange("p (h t) -> p h t", h=H)
        kkT = cat[:, D:2 * D].rearrange("p (h t) -> p h t", h=H)
        catf = cat[:, :].bitcast(F32)
        junk2 = catT[:, 0:8, :].rearrange("p c t -> p (c t)")

        def ld(dst, src, key, eng="sp", **kw):
            P.add(eng, lambda e: e.dma_start(out=dst, in_=src, **kw), w=[key], dma=True, semkey=key)

        ld(identf[:], c_ident, "identf")
        ld(maskT[:], c_maskT, "maskT")
        ld(qdec[:], c_qdec, "qdec")
        ld(kinv[:], c_kinv, "kinv")
        ld(gamP[:], c_gamP, "gamP")
        ld(blockind[:], c_blockind, "blockind")
        P.add("dve", lambda e: e.memset(epst[:], EPS), w=["epst"])
        P.add("dve", lambda e: e.tensor_copy(identb[:], identf[:]), r=["identf"], w=["identb"])

        def precast_weights(l):
            for g in range(7):
                P.add("pool", (lambda g: lambda e: e.dma_start(out=wbf_in[:, g * 1024:(g + 1) * 1024],
                                                               in_=w_in[l, :, g * 1024:(g + 1) * 1024]))(g),
                      w=[("wbf", g)], dma=True, semkey=("wbf", g))
            for g in range(2):
                P.add("pool", (lambda g: lambda e: e.dma_start(out=wbf_out[g * 1024:(g + 1) * 1024, :],
                                                               in_=w_out[l, g * 1024:(g + 1) * 1024, :]))(g),
                      w=[("wbfo", g)], dma=True, semkey=("wbfo", g))

        def load_weights_bf(l, which="all"):
            for g in (range(7) if which in ("all", "in") else []):
                src = wbf_in[:, g * 1024:(g + 1) * 1024].rearrange("(c p) n -> p c n", p=128)
                P.add("pool", (lambda g, src: lambda e: e.dma_start(out=wg[g][:], in_=src))(g, src),
                      r=[("wbf", g)], w=[("wg", g)], dma=True, semkey=("wg", g))
            for g in (range(2) if which in ("all", "out") else []):
                src = wbf_out[g * 1024:(g + 1) * 1024, :].rearrange("(c p) n -> p c n", p=128)
                P.add("pool", (lambda g, src: lambda e: e.dma_start(out=wo[g][:], in_=src))(g, src),
                      r=[("wbfo", g)], w=[("wo", g)], dma=True, semkey=("wo", g))

        def load_weight_group(l, g):
            if g < 7:
                src = w_in[l, :, g * 1024:(g + 1) * 1024].rearrange("(c p) n -> p c n", p=128)
                P.add("pool", lambda e: e.dma_start(out=wg[g][:], in_=src), w=[("wg", g)], dma=True, semkey=("wg", g))
            else:
                src = w_out[l, (g - 7) * 1024:(g - 6) * 1024, :].rearrange("(c p) n -> p c n", p=128)
                P.add("pool", lambda e: e.dma_start(out=wo[g - 7][:], in_=src), w=[("wo", g - 7)], dma=True,
                      semkey=("wo", g - 7))

        deferred_w = []

        def load_weights(l):
            for g in (0, 1, 2):
                load_weight_group(l, g)
            deferred_w.extend([(l, g) for g in (3, 6, 4, 5, 7, 8)])

        def sample_input_loads(l):
            src = xs_in if l == 0 else h1[SEQ:SEQ + NS, :]
            P.add("sp", lambda e: e.dma_start(out=hbuf[0][0:NS, :], in_=src), r=[("h1", "s")] if l else [],
                  w=[("hbuf", 0)], dma=True, semkey=("hbuf", 0))
            P.add("sp", lambda e: e.dma_start(out=rtab[0:NS, :, :], in_=c_ropeS), w=["rtab"], dma=True, semkey="rtab")
            ld(ngT[:], norm_g[l].rearrange("(c p) -> p c", p=128), "ngT", allow_slow_non_contiguous=True)

        def small_loads(l, stg, stgkey):
            ld(bsT[:], gm_b[l].rearrange("h i -> i h"), "bsT", allow_slow_non_contiguous=True)
            ld(w00[:], gm_ws[l, :, 0, 0].partition_broadcast(NS), "w00", allow_slow_non_contiguous=True)
            ld(b00[:], gm_b[l, :, 0].partition_broadcast(NS), "b00", allow_slow_non_contiguous=True)
            ld(lng[:], ln_g[l].partition_broadcast(128), "lng")
            ld(lnb[:], ln_b[l].partition_broadcast(128), "lnb")
            if stg is not None:
                stage_ws(l, stg, stgkey)

        def stage_ws(l, stg, stgkey):
            keys = stgkey if isinstance(stgkey, list) else [stgkey]
            P.add("sp", lambda e: e.dma_start(out=stg[:].rearrange("p (h j) -> p h j", h=H),
                                              in_=gm_ws[l].rearrange("h i j -> i h j")),
                  w=keys, dma=True, semkey=keys[0])

        def wt_prep(l, stg, stgkey):
            p = yield from alloc()

            def tr(e):
                for h in range(H):
                    i = e.transpose(ps[:, p, h * 128:(h + 1) * 128], stg[:, h * 128:(h + 1) * 128], identf[:])
                return i
            yield P.add("pe", tr, r=(stgkey if isinstance(stgkey, list) else [stgkey]) + ["identf"], w=[("ps", p)])
            yield P.add("dve", lambda e: e.tensor_tensor(
                out=WT[:], in0=ps[:, p, :].rearrange("p (h i) -> p h i", h=H),
                in1=maskT[:].unsqueeze(1).to_broadcast([128, H, 128]), op=ALU.mult),
                r=[("ps", p), "maskT"], w=["WT"])
            free(p)

        def rmsnorm_to_xnT(n, hb, hkey, split=False):
            yield P.add("act", lambda e: e.activation(out=xs[0:n, :], in_=hb[0:n, :], func=AF.Square, accum_out=ssq[0:n, :]),
                        r=[hkey], w=["tmpB", "ssq"])
            yield P.add("act", lambda e: e.activation(out=ssq[0:n, :], in_=ssq[0:n, :], func=AF.Sqrt, bias=epst[0:n, :], scale=1.0 / D),
                        r=["ssq", "epst"], w=["ssq"])
            yield P.add("dve", lambda e: e.reciprocal(rstd[0:n, :], ssq[0:n, :]), r=["ssq"], w=["rstd"])
            yield P.add("act", lambda e: e.activation(out=xs[0:n, :], in_=hb[0:n, :], func=AF.Copy, scale=rstd[0:n, :]),
                        r=[hkey, "rstd"], w=["tmpB"])
            if split:
                yield "seg"
            p = yield from alloc()
            pb = PSb(p)

            def tr(e):
                for c in range(8):
                    i = e.transpose(pb[:, c * 128:c * 128 + n], xs[0:n, c * 128:(c + 1) * 128], identb[0:n, 0:n])
                return i
            yield P.add("pe", tr, r=["tmpB", "identb"], w=[("ps", p)])
            yield P.add("dve", lambda e: e.tensor_tensor(
                out=xnT[:, :, 0:n], in0=pb[:, 0:1024].rearrange("p (c t) -> p c t", c=8)[:, :, 0:n],
                in1=ngT[:].unsqueeze(2).to_broadcast([128, 8, n]), op=ALU.mult),
                r=[("ps", p), "ngT"], w=["xnT"])
            free(p)

        def proj(g, n):
            p = yield from alloc()

            def mm(e):
                for half in range(2):
                    for c in range(8):
                        i = e.matmul(ps[0:n, p, half * 512:(half + 1) * 512], xnT[:, c, 0:n],
                                     wg[g][:, c, half * 512:(half + 1) * 512], start=(c == 0), stop=(c == 7))
                return i
            yield P.add("pe", mm, r=["xnT", ("wg", g)], w=[("ps", p)])
            return p

        def rope(p, n, dst, dstkey, dst32=None):
            src16 = ps[0:n, p, :].rearrange("p (a d) -> p a d", d=64)
            src4 = ps[0:n, p, :].rearrange("p (h t d) -> p h t d", h=H, t=2)
            t4 = tmpB[0:n, :].rearrange("p (h t d) -> p h t d", h=H, t=2)

            def t2(e):
                e.tensor_tensor(out=t4[:, :, 0, :], in0=src4[:, :, 1, :],
                                in1=rtab[0:n, 2, :].unsqueeze(1).to_broadcast([n, H, 64]), op=ALU.mult)
                return e.tensor_tensor(out=t4[:, :, 1, :], in0=src4[:, :, 0, :],
                                       in1=rtab[0:n, 1, :].unsqueeze(1).to_broadcast([n, H, 64]), op=ALU.mult)
            yield P.add("dve", t2, r=[("ps", p), "rtab"], w=["tmpB"])
            yield P.add("dve", lambda e: e.tensor_tensor(
                out=src16, in0=src16, in1=rtab[0:n, 0, :].unsqueeze(1).to_broadcast([n, 16, 64]), op=ALU.mult),
                r=[("ps", p), "rtab"], w=[("ps", p)])
            yield P.add("dve", lambda e: e.tensor_tensor(out=dst[0:n, :], in0=ps[0:n, p, :], in1=tmpB[0:n, :], op=ALU.add),
                        r=[("ps", p), "tmpB"], w=[dstkey])
            if dst32 is not None:
                yield P.add("dve", lambda e: e.tensor_tensor(out=tmpB[0:32, :], in0=ps[0:32, p, :], in1=tmpB[0:32, :], op=ALU.add),
                            r=[("ps", p), "tmpB"], w=["tmpB"])
                px = yield from alloc()

                def tr32(e):
                    for h in range(H):
                        i = e.transpose(ps[:, px, h * 32:(h + 1) * 32], tmpB[0:32, h * 128:(h + 1) * 128], identf[0:32, 0:32])
                    return i
                yield P.add("pe", tr32, r=["tmpB", "identf"], w=[("ps", px)])
                yield P.add("act", lambda e: e.activation(out=dst32, in_=ps[:, px, 0:256].rearrange("p (h t) -> p h t", h=H),
                                                          func=AF.Copy), r=[("ps", px)], w=vb2keys)
                free(px)
            free(p, late=True)

        def transpose_heads(src, srckey, n, dst, dstkey):
            p = yield from alloc()
            pb = PSb(p)

            def tr(e):
                for h in range(H):
                    i = e.transpose(pb[:, h * 128:h * 128 + n], src[0:n, h * 128:(h + 1) * 128], identb[0:n, 0:n])
                return i
            yield P.add("pe", tr, r=[srckey, "identb"], w=[("ps", p)])
            yield P.add("act", lambda e: e.activation(
                out=dst[:, :, 0:n], in_=pb[:, 0:1024].rearrange("p (h t) -> p h t", h=H)[:, :, 0:n], func=AF.Copy),
                r=[("ps", p)], w=[dstkey])
            free(p)

        def layernorm_vg(p, n, sample, l):
            def stats(e):
                e.bn_stats(bst[0:n, 0:6], ps[0:n, p, 0:512])
                return e.bn_stats(bst[0:n, 6:12], ps[0:n, p, 512:1024])
            yield P.add("dve", stats, r=[("ps", p)], w=["bst"])
            yield P.add("dve", lambda e: e.bn_aggr(mv[0:n, :], bst[0:n, :]), r=["bst"], w=["mv"])
            yield P.add("act", lambda e: e.activation(out=lsd[0:n, :], in_=mv[0:n, 1:2], func=AF.Sqrt, bias=epst[0:n, :], scale=1.0),
                        r=["mv", "epst"], w=["lsd"])
            yield P.add("dve", lambda e: e.reciprocal(lsd[0:n, :], lsd[0:n, :]), r=["lsd"], w=["lsd"])
            yield P.add("dve", lambda e: e.tensor_scalar(out=nmr[0:n, :], in0=mv[0:n, 0:1], scalar1=lsd[0:n, 0:1], scalar2=-1.0,
                                                         op0=ALU.mult, op1=ALU.mult), r=["mv", "lsd"], w=["nmr"])
            yield P.add("act", lambda e: e.activation(out=tmpB[0:n, :], in_=ps[0:n, p, :], func=AF.Identity,
                                                      bias=nmr[0:n, :], scale=lsd[0:n, :]),
                        r=[("ps", p), "nmr", "lsd"], w=["tmpB"])
            free(p)
            yield P.add("pool", lambda e: e.tensor_tensor(out=tmpB[0:n, :], in0=tmpB[0:n, :], in1=lng[0:n, :], op=ALU.mult),
                        r=["tmpB", "lng"], w=["tmpB"])
            if sample:
                yield P.add("pool", lambda e: e.tensor_tensor(out=tmpB[0:n, :], in0=tmpB[0:n, :], in1=lnb[0:n, :], op=ALU.add),
                            r=["tmpB", "lnb"], w=["tmpB"])
                yield P.add("sp", lambda e: e.dma_start(out=gv_out[l], in_=tmpB[0:n, :]), r=["tmpB"], w=[("gv", l)],
                            dma=True, semkey="tmpB")
            else:
                yield P.add("pool", lambda e: e.tensor_tensor(out=vn[0:n, :], in0=tmpB[0:n, :], in1=lnb[0:n, :], op=ALU.add),
                            r=["tmpB", "lnb"], w=["vn"])

        def groupnorm_gate(po, n, sgt, sgkey, src=None, srckey=None):
            if src is None:
                src = lambda h: ps[0:n, po, h * 128:(h + 1) * 128]
                srckey = ("ps", po)

            def sq(e):
                for h in range(H):
                    i = e.activation(out=junk[0:n, :], in_=src(h), func=AF.Square,
                                     accum_out=oss[0:n, h:h + 1])
                return i
            yield P.add("act", sq, r=[srckey], w=["oss", ("osb", 0)])
            yield P.add("act", lambda e: e.activation(out=oss[0:n, :], in_=oss[0:n, :], func=AF.Sqrt, bias=epst[0:n, :], scale=1.0 / 128),
                        r=["oss", "epst"], w=["oss"])
            yield P.add("dve", lambda e: e.reciprocal(orstd[0:n, :], oss[0:n, :]), r=["oss"], w=["orstd"])

            def gate(e):
                for h in range(H):
                    i = e.scalar_tensor_tensor(out=cat[0:n, h * 128:(h + 1) * 128], in0=src(h),
                                               scalar=orstd[0:n, h:h + 1], in1=sgt[0:n, h * 128:(h + 1) * 128],
                                               op0=ALU.mult, op1=ALU.mult)
                return i
            yield P.add("dve", gate, r=[srckey, "orstd", sgkey], w=["cat_o"])
            if po is not None:
                free(po, late=True)

        def out_proj_residual(n, hb, hkey, seg=False):
            p = yield from alloc()
            pb = PSb(p)

            def tr(e):
                for c in range(16):
                    i = e.transpose(pb[:, c * 128:c * 128 + n], cat[0:n, c * 128:(c + 1) * 128], identb[0:n, 0:n])
                return i
            yield P.add("pe", tr, r=["cat_o", "cat_m", "identb"], w=[("ps", p)])
            yield P.add("act", lambda e: e.activation(
                out=catT[:, :, 0:n], in_=pb[:, :].rearrange("p (c t) -> p c t", c=16)[:, :, 0:n], func=AF.Copy),
                r=[("ps", p)], w=["catT"])
            free(p)
            if seg:
                yield "seg"
            py = yield from alloc()

            def mm(e):
                for half in range(2):
                    for c in range(16):
                        i = e.matmul(ps[0:n, py, half * 512:(half + 1) * 512], catT[:, c, 0:n],
                                     wo[c // 8][:, c % 8, half * 512:(half + 1) * 512], start=(c == 0), stop=(c == 15))
                return i
            yield P.add("pe", mm, r=["catT", ("wo", 0), ("wo", 1)], w=[("ps", py)])
            yield P.add("dve", lambda e: e.tensor_tensor(out=hb[0:n, :], in0=ps[0:n, py, :], in1=hb[0:n, :], op=ALU.add),
                        r=[("ps", py), hkey], w=[hkey])
            free(py)

        def final_norm_store(n, hb, hkey, dst, dstkey):
            yield P.add("act", lambda e: e.activation(out=junk2[0:n, :], in_=hb[0:n, :], func=AF.Square, accum_out=ssq2[0:n, :]),
                        r=[hkey], w=["catT", "ssq2"])
            yield P.add("act", lambda e: e.activation(out=ssq2[0:n, :], in_=ssq2[0:n, :], func=AF.Sqrt, bias=epst[0:n, :], scale=1.0 / D),
                        r=["ssq2", "epst"], w=["ssq2"])
            yield P.add("dve", lambda e: e.reciprocal(rstd2[0:n, :], ssq2[0:n, :]), r=["ssq2"], w=["rstd2"])
            yield P.add("dve", lambda e: e.scalar_tensor_tensor(out=hb[0:n, :], in0=hb[0:n, :], scalar=rstd2[0:n, 0:1],
                                                                in1=fing[0:n, :], op0=ALU.mult, op1=ALU.mult),
                        r=[hkey, "rstd2", "fing"], w=[hkey])
            yield P.add("pool", lambda e: e.dma_start(out=dst, in_=hb[0:n, :]), r=[hkey], w=[dstkey], dma=True, semkey=hkey)

        ver = {}

        def setv(key, c, sample):
            ver[key] = ("s" if sample else c)

        def chk(keys, c):
            for k in keys:
                assert ver.get(k) == c, ("stale/early buffer", k, ver.get(k), c)

        def bufs(c, sample):
            par = 0 if sample else (c + 1) % 3
            return (hbuf[par], ("hbuf", par), vb[0], ("vb", 0), sg[0], ("sg", 0))

        def stageA(l, c, sample, part=0):
            n = NS if sample else 128
            hb, hkey, vbt, vkey, sgt, sgkey = bufs(c, sample)
            if l == 0:
                src = xs_in if sample else xp[c * 128:(c + 1) * 128, :]
            else:
                src = h1[SEQ:SEQ + NS, :] if sample else h1[c * 128:(c + 1) * 128, :]
            srckey = ("h1", "s" if sample else c)

            def seg_norm(cc=c):
                hb2, hkey2 = bufs(cc, sample)[0:2]
                if l == 0:
                    src2 = xs_in if sample else xp[cc * 128:(cc + 1) * 128, :]
                else:
                    src2 = h1[SEQ:SEQ + NS, :] if sample else h1[cc * 128:(cc + 1) * 128, :]
                srckey2 = ("h1", "s" if sample else cc)
                if not sample:
                    yield P.add("sp", lambda e: e.dma_start(out=hb2[0:n, :], in_=src2), r=[srckey2] if l else [], w=[hkey2],
                                dma=True, semkey=hkey2)
                    rsrc = c_ropeS if sample else c_ropeP[cc]
                    yield P.add("sp", lambda e: e.dma_start(out=rtab[0:n, :, :], in_=rsrc), w=["rtab"], dma=True, semkey="rtab")
                yield from rmsnorm_to_xnT(n, hb2, hkey2, split=(part == 0 and cc != c))

            def seg_q():
                pq = yield from proj(0, n)
                if not sample:
                    def presq(e):
                        for h in range(H):
                            i = e.activation(out=ps[0:n, pq, h * 128:(h + 1) * 128], in_=ps[0:n, pq, h * 128:(h + 1) * 128],
                                             func=AF.Copy, scale=qdec[0:n, h:h + 1])
                        return i
                    yield P.add("act", presq, r=[("ps", pq), "qdec"], w=[("ps", pq)])
                if part == 0:
                    yield "seg"
                yield from rope(pq, n, q_rot, "q_rot", dst32=(qT32 if (not sample and c == 0) else None))
                setv("q_rot", c, sample)

            def seg_k():
                pk = yield from proj(1, n)
                if sample:
                    yield P.add("act", lambda e: e.activation(out=ps[0:n, pk, :], in_=ps[0:n, pk, :], func=AF.Copy, scale=128.0 ** -0.5),
                                r=[("ps", pk)], w=[("ps", pk)])
                else:
                    def presk(e):
                        for h in range(H):
                            i = e.activation(out=ps[0:n, pk, h * 128:(h + 1) * 128], in_=ps[0:n, pk, h * 128:(h + 1) * 128],
                                             func=AF.Copy, scale=kinv[0:n, h:h + 1])
                        return i
                    yield P.add("act", presk, r=[("ps", pk), "kinv"], w=[("ps", pk)])
                if part == 0:
                    yield "seg"
                yield from rope(pk, n, k_rot, "k_rot", dst32=(kT32 if (not sample and c == 0) else None))
                setv("k_rot", c, sample)

            def seg_v():
                pv = yield from proj(2, n)
                yield P.add("act", lambda e: e.activation(out=vbt[0:n, :], in_=ps[0:n, pv, :], func=AF.Copy), r=[("ps", pv)], w=[vkey])
                free(pv)
                setv(vkey, c, sample)

            def seg_gr():
                pg = yield from proj(3, n)
                yield P.add("act", lambda e: e.activation(out=sgt[0:n, :], in_=ps[0:n, pg, :], func=AF.Silu), r=[("ps", pg)], w=[sgkey])
                free(pg)
                setv(sgkey, c, sample)

            def seg_gg():
                pgg = yield from proj(6, n)
                yield P.add("act", lambda e: e.activation(out=sgg[0:n, :], in_=ps[0:n, pgg, :], func=AF.Silu), r=[("ps", pgg)], w=["sgg"])
                free(pgg)

            def seg_u():
                pu = yield from proj(4, n)
                yield P.add("dve", lambda e: e.tensor_tensor(out=sgg[0:n, :], in0=ps[0:n, pu, :], in1=sgg[0:n, :], op=ALU.mult),
                            r=[("ps", pu), "sgg"], w=["sgg"])
                free(pu)
                setv("sgg", c, sample)

            def seg_vg():
                pvg = yield from proj(5, n)
                yield from layernorm_vg(pvg, n, sample, l)
                setv("vn", c, sample)

            if part == 0:
                segs = [seg_q, seg_k, seg_v, seg_vg, seg_gr, seg_gg]
                if c + 1 < NCH:
                    nrm = seg_norm(c + 1)

                    def nrm_a():
                        for r in nrm:
                            if r == "seg":
                                return
                            yield r

                    def nrm_b():
                        yield from nrm
                    segs += [nrm_a, seg_u, nrm_b]
                else:
                    segs.append(seg_u)
            elif part == 3:
                segs = [seg_norm]
            elif part == 1:
                segs = [seg_norm, seg_q, seg_k, seg_v]
            else:
                segs = [seg_gr, seg_gg, seg_u, seg_vg]
            for i, sgm in enumerate(segs):
                yield from sgm()
                if i + 1 < len(segs):
                    yield "seg"

        def stageB(l, c):
            n = 128
            hb, hkey, vbt, vkey, sgt, sgkey = bufs(c, False)
            ptq = yield from alloc()
            pbq = PSb(ptq)

            def trqk(e):
                for h in range(H):
                    i = e.transpose(pbq[:, h * 128:(h + 1) * 128], q_rot[:, h * 128:(h + 1) * 128], identb[:])
                for h in range(H):
                    i = e.transpose(pbq[:, D + h * 128:D + (h + 1) * 128], k_rot[:, h * 128:(h + 1) * 128], identb[:])
                return i
            chk(["q_rot", "k_rot"], c)
            yield P.add("pe", trqk, r=["q_rot", "k_rot", "identb"], w=[("ps", ptq)])
            yield P.add("act", lambda e: e.activation(out=cat[:, :], in_=pbq[:, :], func=AF.Copy),
                        r=[("ps", ptq)], w=["cat_o", "cat_m"])
            free(ptq)
            yield "seg"
            psc = yield from alloc()

            def mm_sc(e):
                for h in range(H):
                    i = e.matmul(ps[:, psc, h * 128:(h + 1) * 128], kkT[:, h, :], qdT[:, h, :], start=True, stop=True)
                    if c == 0:
                        i = e.matmul(ps[0:32, psc, h * 128:h * 128 + 32], kT32[:, h, :], qT32[:, h, :], start=True, stop=True)
                return i
            yield P.add("pe", mm_sc, r=["cat_m", "cat_o"] + (vb2keys if c == 0 else []), w=[("ps", psc)])
            yield P.add("dve", lambda e: e.tensor_tensor(
                out=scT, in0=ps[:, psc, :].rearrange("p (h i) -> p h i", h=H),
                in1=maskT[:].unsqueeze(1).to_broadcast([128, H, 128]), op=ALU.mult),
                r=[("ps", psc), "maskT"], w=["catT"])
            free(psc)
            yield "seg"
            po = yield from alloc()

            def mm_o(e):
                for h in range(H):
                    i = e.matmul(ps[:, po, h * 128:(h + 1) * 128], scT[:, h, :], vbt[:, h * 128:(h + 1) * 128],
                                 start=True, stop=(c == 0))
                    if c > 0:
                        i = e.matmul(ps[:, po, h * 128:(h + 1) * 128], qdT[:, h, :], Sbf[:, h, :], start=False, stop=True)
                return i
            chk([vkey], c)
            yield P.add("pe", mm_o, r=["catT", vkey, "cat_o"] + (["Sbf"] if c > 0 else []), w=[("ps", po)])
            yield from groupnorm_gate(po, n, sgt, sgkey)
            pkv = yield from alloc()

            def mm_kv(e):
                for h in range(H):
                    i = e.matmul(ps[:, pkv, h * 128:(h + 1) * 128], k_rot[:, h * 128:(h + 1) * 128],
                                 vbt[:, h * 128:(h + 1) * 128], start=True, stop=True)
                return i
            chk(["k_rot", vkey], c)
            yield P.add("pe", mm_kv, r=["k_rot", vkey], w=[("ps", pkv)])
            if c == 0:
                yield P.add("dve", lambda e: e.tensor_copy(S32[:], ps[:, pkv, :]), r=[("ps", pkv)], w=["S32"])
            else:
                def upd(e):
                    for h in range(H):
                        i = e.scalar_tensor_tensor(out=S32[:, h * 128:(h + 1) * 128], in0=S32[:, h * 128:(h + 1) * 128],
                                                   scalar=_C["sdec"][h], in1=ps[:, pkv, h * 128:(h + 1) * 128],
                                                   op0=ALU.mult, op1=ALU.add)
                    return i
                yield P.add("dve", upd, r=[("ps", pkv), "S32"], w=["S32"])
            free(pkv)
            yield "seg"
            pgm = yield from alloc()

            def mm_g(e):
                for h in range(H):
                    i = e.matmul(ps[:, pgm, h * 128:(h + 1) * 128], WT[:, h, :], vn[:, h * 128:(h + 1) * 128],
                                 start=True, stop=True)
                return i
            chk(["vn", "sgg", sgkey], c)
            yield P.add("pe", mm_g, r=["WT", "vn"], w=[("ps", pgm)])

            def gm(e):
                for h in range(H):
                    i = e.scalar_tensor_tensor(out=cat[0:n, D + h * 128:D + (h + 1) * 128],
                                               in0=ps[0:n, pgm, h * 128:(h + 1) * 128], scalar=bsT[0:n, h:h + 1],
                                               in1=sgg[0:n, h * 128:(h + 1) * 128], op0=ALU.add, op1=ALU.mult)
                return i
            yield P.add("dve", gm, r=[("ps", pgm), "bsT", "sgg"], w=["cat_m"])
            free(pgm, late=True)
            yield "seg"
            yield from out_proj_residual(n, hb, hkey, seg=True)
            if l == 0:
                dst = h1[c * 128:(c + 1) * 128, :]
                yield P.add("pool", lambda e: e.dma_start(out=dst, in_=hb[0:n, :]), r=[hkey], w=[("h1", c)], dma=True, semkey=hkey)
            else:
                yield from final_norm_store(n, hb, hkey, yp[c * 128:(c + 1) * 128, :], ("yp", c))
            if c < NCH - 1:
                def sbf(e):
                    for h in range(H):
                        i = e.activation(out=Sbf[:, h, :], in_=S32[:, h * 128:(h + 1) * 128], func=AF.Copy,
                                         scale=_C["sdec"][h])
                    return i
                yield P.add("act", sbf, r=["S32"], w=["Sbf"])
            else:
                def sfin(e):
                    for h in range(H):
                        i = e.activation(out=S32[:, h * 128:(h + 1) * 128], in_=S32[:, h * 128:(h + 1) * 128],
                                         func=AF.Copy, scale=_C["sdec"][h])
                    return i
                yield P.add("act", sfin, r=["S32"], w=["S32"])
                yield P.add("sp", lambda e: e.dma_start(out=sp_out[l].rearrange("h d v -> d h v"),
                                                        in_=S32[:].rearrange("p (h v) -> p h v", h=H)),
                            r=["S32"], w=[("spo", l)], dma=True, semkey="S32")

        def sample_relayout(l):
            n = NS
            hb, hkey, vbt, vkey, sgt, sgkey = bufs(0, True)
            yield P.add("sp", lambda e: e.dma_start(out=qkvscr[2], in_=vbt[0:n, :]), r=[vkey], w=[("scr", 2)],
                        dma=True, semkey=vkey)
            for t, key, dst, dkey in [(q_rot, "q_rot", q2all, "q2all"), (k_rot, "k_rot", k2all, "k2all")]:
                p = yield from alloc()
                pb = PSb(p)
                tv8 = t[0:n, :].rearrange("b (p dl) -> b dl p", dl=8)

                def tr(e, pb=pb, tv8=tv8):
                    for dl in range(8):
                        i = e.transpose(pb[:, dl * NS:(dl + 1) * NS], tv8[:, dl, :], identb[0:n, 0:n])
                    return i
                yield P.add("pe", tr, r=[key, "identb"], w=[("ps", p)])
                yield P.add("act", (lambda pb, dst: lambda e: e.activation(
                    out=dst[:].rearrange("p b dl -> p dl b"), in_=pb[:, 0:8 * NS].rearrange("p (dl b) -> p dl b", dl=8),
                    func=AF.Copy))(pb, dst), r=[("ps", p)], w=[dkey])
                free(p)

        def sample_loop(l):
            n = NS
            hb, hkey, vbt, vkey, sgt, sgkey = bufs(0, True)
            stbuf = [(S32, "S32"), (hbuf[1], ("hbuf", 1)), (hbuf[2], ("hbuf", 2)), (tmpB, "tmpB")]

            def front(b):
                Sb, Skey = stbuf[b % 4]
                pre = (l, b) in preloaded
                if b % 4 == 0:
                    g0 = (b // 4) % 2
                    for h in range(H):
                        yield P.add("sp", (lambda h: lambda e: e.dma_start(
                            out=vb2[h * 16:(h + 1) * 16, g0 * 4:(g0 + 1) * 4, :],
                            in_=qkvscr[2, b:b + 4, h * 128:(h + 1) * 128].partition_broadcast(16)))(h),
                            r=[("scr", 2)], w=[("vb2", h, g0)], dma=True, semkey=("vb2", h, g0))
                if not pre:
                    yield P.add("sp", lambda e: e.dma_start(
                        out=Sb[:], in_=st_in[l, b].rearrange("h (dh dl) v -> (h dh) (dl v)", dl=8)),
                        w=[Skey], dma=True, semkey=Skey)
                yield P.add("act", lambda e: e.activation(out=Sb[:], in_=Sb[:], func=AF.Copy, scale=gamP[:, 0:1]),
                            r=[Skey, "gamP"], w=[Skey])

                def upd(e):
                    for dl in range(8):
                        i = e.scalar_tensor_tensor(out=Sb[:, dl * 128:(dl + 1) * 128], in0=vb2[:, b % 8, :],
                                                   scalar=k2all[:, b, dl:dl + 1], in1=Sb[:, dl * 128:(dl + 1) * 128],
                                                   op0=ALU.mult, op1=ALU.add)
                    return i
                yield P.add("dve", upd, r=[Skey, "k2all"] + [("vb2", h, (b // 4) % 2) for h in range(H)], w=[Skey])
                qm = Q2[b % 2]
                yield P.add("dve", lambda e: e.tensor_tensor(
                    out=qm[:], in0=q2all[:, b, :].unsqueeze(2).to_broadcast([128, 8, H]),
                    in1=blockind[:].unsqueeze(1).to_broadcast([128, 8, H]), op=ALU.mult),
                    r=["q2all", "blockind"], w=[("Q2", b % 2)])

            pobs = {}

            def back(b):
                Sb, Skey = stbuf[b % 4]
                qm = Q2[b % 2]
                qmkey = ("Q2", b % 2)
                Sbv = Sbf[:].rearrange("p h v -> p (h v)")
                yield P.add("act", lambda e: e.activation(out=Sbv, in_=Sb[:], func=AF.Copy), r=[Skey], w=["Sbf"])
                yield P.add("act", lambda e: e.dma_start(
                    out=ss_out[l, b].rearrange("h (dh dl) v -> (h dh) (dl v)", dl=8), in_=Sb[:]),
                    r=[Skey], w=[("sso", l, b)], dma=True, semkey=Skey)
                pob = yield from alloc()

                def mm_os(e):
                    for dl in range(8):
                        i = e.matmul(ps[0:H, pob, 0:128], qm[:, dl, :], Sbv[:, dl * 128:(dl + 1) * 128],
                                     start=(dl == 0), stop=(dl == 7))
                    return i
                yield P.add("pe", mm_os, r=[qmkey, "Sbf"], w=[("ps", pob)])
                pobs[b] = pob

            def tail(b):
                pob = pobs.pop(b)
                ob = osb[0]
                obkey = ("osb", 0)
                yield P.add("act", lambda e: e.activation(out=ob, in_=ps[0:H, pob, 0:128], func=AF.Copy),
                            r=[("ps", pob)], w=[obkey])
                free(pob)
                yield P.add("act", lambda e: e.dma_start(out=oscr[b].rearrange("(h v) -> h v", h=H), in_=ob),
                            r=[obkey], w=[("oscr", b)], dma=True, semkey=obkey)

            yield from front(0)
            yield from front(1)
            for b in range(NS):
                yield from back(b)
                if b >= 1:
                    yield from tail(b - 1)
                if b + 2 < NS:
                    yield from front(b + 2)
                if deferred_w and b % 2 == 1:
                    load_weight_group(*deferred_w.pop(0))
            yield from tail(NS - 1)
            while deferred_w:
                load_weight_group(*deferred_w.pop(0))
            yield P.add("sp", lambda e: e.dma_start(out=hbuf[2][0:n, :], in_=oscr),
                        r=[("oscr", b) for b in range(NS)], w=[("hbuf", 2)], dma=True, semkey=("hbuf", 2))

        def sample_stream(l):
            n = NS
            hb, hkey, vbt, vkey, sgt, sgkey = bufs(0, True)
            yield from stageA(l, 0, True, part=1)
            yield from sample_relayout(l)
            if l != 0:
                yield from wt_prep(l, catf, ["cat_o", "cat_m"])
            if l == 0:
                yield from sample_loop(l)
            yield from stageA(l, 0, True, part=2)
            tv = tmpB[0:n, :].rearrange("p (h g) -> p h g", h=H)
            yield P.add("dve", lambda e: e.tensor_tensor(out=tv, in0=tv, in1=w00[:].unsqueeze(2).to_broadcast([n, H, 128]),
                                                         op=ALU.mult), r=["tmpB", "w00"], w=["tmpB"])
            yield P.add("dve", lambda e: e.tensor_tensor(out=tv, in0=tv, in1=b00[:].unsqueeze(2).to_broadcast([n, H, 128]),
                                                         op=ALU.add), r=["tmpB", "b00"], w=["tmpB"])
            yield P.add("dve", lambda e: e.tensor_tensor(out=cat[0:n, D:2 * D], in0=tmpB[0:n, :], in1=sgg[0:n, :], op=ALU.mult),
                        r=["tmpB", "sgg"], w=["cat_m"])
            if l != 0:
                yield from sample_loop(l)

        def sample_tail(l):
            n = NS
            hb, hkey, vbt, vkey, sgt, sgkey = bufs(0, True)
            yield from groupnorm_gate(None, n, sgt, sgkey, src=lambda h: hbuf[2][0:n, h * 128:(h + 1) * 128],
                                      srckey=("hbuf", 2))
            yield "seg"
            yield from out_proj_residual(n, hb, hkey, seg=True)
            if l == 0:
                yield P.add("sp", lambda e: e.dma_start(out=h1[SEQ:SEQ + NS, :], in_=hb[0:n, :]), r=[hkey], w=[("h1", "s")],
                            dma=True, semkey=hkey)
            else:
                yield from final_norm_store(n, hb, hkey, ys, ("ys",))

        def drive(streams):
            prog = [0.0] * len(streams)
            alive = [True] * len(streams)
            while any(alive):
                order = sorted([i for i in range(len(streams)) if alive[i]], key=lambda i: prog[i])
                stepped = False
                for i in order:
                    try:
                        r = next(streams[i][0])
                    except StopIteration:
                        alive[i] = False
                        stepped = True
                        break
                    if r == "blocked":
                        continue
                    if r == "seg":
                        stepped = True
                        break
                    prog[i] += 1.0 / streams[i][1]
                    stepped = True
                    break
                assert stepped, "all streams blocked on PSUM allocation"

        def drive_script(ga, gb, pattern):
            gens = {"A": ga, "B": gb}
            done = {"A": False, "B": False}
            for who in pattern:
                if done[who]:
                    continue
                while True:
                    try:
                        r = next(gens[who])
                    except StopIteration:
                        done[who] = True
                        break
                    assert r != "blocked", "PSUM alloc blocked in scripted merge"
                    if r == "seg":
                        break
            for who in ("A", "B"):
                if not done[who]:
                    for r in gens[who]:
                        assert r != "blocked"

        preloaded = set()

        def preload_states(l):
                for b, (Sb, Skey) in enumerate([(S32, "S32"), (hbuf[1], ("hbuf", 1)), (hbuf[2], ("hbuf", 2))]):
                    P.add("sp", (lambda Sb, b, l: lambda e: e.dma_start(
                        out=Sb[:], in_=st_in[l, b].rearrange("h (dh dl) v -> (h dh) (dl v)", dl=8)))(Sb, b, l),
                        w=[Skey], dma=True, semkey=Skey)
                    preloaded.add((l, b))

        for l in range(DEPTH):
            if l == 0:
                sample_input_loads(l)
                preload_states(l)
                small_loads(l, fing, "fing")
                load_weights(l)
            else:
                small_loads(l, None, None)
                preload_states(l)
            drive([[sample_stream(l), 1]])
            if l == 0:
                drive([[wt_prep(l, fing, "fing"), 1]])
                ld(fing[:], fin_g.partition_broadcast(128), "fing")

            def first_A(l=l):
                yield from stageA(l, 0, False, part=3)
                yield "seg"
                yield from stageA(l, 0, False)
            drive_script(first_A(), sample_tail(l), ["B", "A", "A", "A", "B", "A", "A", "B"] + ["A"] * 12)
            for c in range(NCH):
                if l == 0 and c == 2:
                    precast_weights(1)
                if c + 1 < NCH:
                    drive_script(stageA(l, c + 1, False), stageB(l, c), PATTERN)
                else:
                    if l + 1 < DEPTH:
                        sample_input_loads(l + 1)
                        load_weights_bf(l + 1, "in")
                    drive([[stageB(l, c), 1]])
                    if l + 1 < DEPTH:
                        load_weights_bf(l + 1, "out")
                        stage_ws(l + 1, catf, ["cat_o", "cat_m"])
        outkeys = [("ys",)] + [("yp", c) for c in range(NCH)] + [("spo", l) for l in range(DEPTH)] + \
                  [("sso", l, b) for l in range(DEPTH) for b in range(NS)] + [("gv", l) for l in range(DEPTH)]
        P.add("sp", lambda e: e.nop(), r=outkeys)
        P.emit(st)
        nc._prog = P
    return nc


PATTERN = ["B", "A", "B", "A", "A", "B", "A", "A", "B", "A", "A", "B", "A", "A", "B", "A", "A"]

_NC_CACHE = {}


def kernel(x_prompt, x_sample, state_ret, norm_g, w_in, w_out, gm_ws, gm_b, gm_ln_g, gm_ln_b, final_g):
    f = lambda a: np.ascontiguousarray(np.asarray(a, dtype=np.float32))
    x_prompt, x_sample, state_ret = f(x_prompt), f(x_sample), f(state_ret)
    shared = dict(norm_g=f(norm_g), w_in=f(w_in), w_out=f(w_out), gm_ws=f(gm_ws), gm_b=f(gm_b), ln_g=f(gm_ln_g),
                  ln_b=f(gm_ln_b), fin_g=f(final_g), c_qdec=_C["qdec"], c_kinv=_C["kinv"], c_ropeP=_C["ropeP"],
                  c_ropeS=_C["ropeS"], c_maskT=_C["maskT"], c_ident=_C["ident"], c_gamP=_C["gamP"], c_blockind=_C["blockind"])
    in_maps = []
    for c in range(NCORES):
        m = dict(shared)
        m["xp"] = x_prompt[c]
        m["xs"] = np.ascontiguousarray(x_sample[c * NS:(c + 1) * NS, 0, :])
        m["st"] = np.ascontiguousarray(state_ret[:, c * NS:(c + 1) * NS])
        in_maps.append(m)
    if "nc" not in _NC_CACHE:
        _NC_CACHE["nc"] = build_nc()
    res = run_bass_kernel_spmd(_NC_CACHE["nc"], in_maps, core_ids=list(range(NCORES)))
    R = res.results
    y_prompt = np.stack([R[c]["yp"] for c in range(NCORES)], 0)
    y_sample = np.concatenate([R[c]["ys"] for c in range(NCORES)], 0)[:, None, :]
    sp = np.stack([R[c]["sp_out"] for c in range(NCORES)], 1)
    ss = np.concatenate([R[c]["ss_out"] for c in range(NCORES)], 1)
    gv = np.concatenate([R[c]["gv_out"] for c in range(NCORES)], 1).reshape(DEPTH, NCORES * NS, 1, H, 128)
    return (y_prompt.astype(np.float32), y_sample.astype(np.float32), sp.astype(np.float32),
            ss.astype(np.float32), gv.astype(np.float32))
```

```python
import numpy as np
from contextlib import ExitStack
import concourse.bass as bass
import concourse.mybir as mybir
from concourse.bass_utils import run_bass_kernel_spmd

F32 = mybir.dt.float32
BF16 = mybir.dt.bfloat16
AF = mybir.ActivationFunctionType
ALU = mybir.AluOpType

D = 1024
SEQ = 2048
NCH = 16
DEPTH = 2
NS = 16
H = 8
PAST = 16384
EPS = 1e-6
NCORES = 8


class _Op:
    __slots__ = ("eng", "fn", "deps", "dma", "semkey", "sig", "waits", "r", "w")

    def __init__(self, eng, fn, deps, dma, semkey):
        self.eng, self.fn, self.deps, self.dma, self.semkey = eng, fn, deps, dma, semkey
        self.sig = None
        self.waits = []


class Prog:
    def __init__(self, nc):
        self.nc = nc
        self.ops = []
        self.last_w = {}
        self.readers = {}

    def add(self, eng, fn, r=(), w=(), dma=False, semkey=None):
        i = len(self.ops)
        deps = {}
        for k in r:
            j = self.last_w.get(k)
            if j is not None:
                deps[j] = True
        for k in w:
            j = self.last_w.get(k)
            if j is not None:
                deps.setdefault(j, False)
            for j in self.readers.get(k, ()):
                deps.setdefault(j, False)
        for k in r:
            self.readers.setdefault(k, []).append(i)
        for k in w:
            self.last_w[k] = i
            self.readers[k] = []
        if dma:
            assert semkey is not None
        op = _Op(eng, fn, deps, dma, semkey)
        op.r, op.w = list(r), list(w)
        self.ops.append(op)
        return i

    def emit(self, stack):
        nc = self.nc
        ops = self.ops
        for i, op in enumerate(ops):
            for j, raw in op.deps.items():
                p = ops[j]
                if p.dma:
                    need = True
                elif p.eng == op.eng:
                    if op.dma:
                        need = True
                    elif p.eng == "pe":
                        need = False
                    else:
                        need = True
                else:
                    need = True
                if need:
                    op.waits.append(j)
                    p.sig = True
        cnt = {}
        sems = {}
        for op in ops:
            if not op.sig:
                continue
            k = ("d", op.semkey) if op.dma else ("e", op.eng)
            cnt[k] = cnt.get(k, 0) + (16 if op.dma else 1)
            op.sig = (k, cnt[k])
            sems[k] = None
        for n, k in enumerate(sems):
            sems[k] = stack.enter_context(nc.semaphore("sem%d" % n))
        self.nsems = len(sems)
        block = stack.enter_context(nc.Block())

        def replay(engname, e):
            waited = {}
            for op in ops:
                if op.eng != engname:
                    continue
                need = {}
                for j in op.waits:
                    k, v = ops[j].sig
                    if waited.get(k, 0) >= v:
                        continue
                    need[k] = max(need.get(k, 0), v)
                for k, v in need.items():
                    e.wait_ge(sems[k], v)
                    waited[k] = v
                inst = op.fn(e)
                if op.sig:
                    inst.then_inc(sems[op.sig[0]], 16 if op.dma else 1)

        @block.sync
        def _(e):
            replay("sp", e)

        @block.tensor
        def _(e):
            replay("pe", e)

        @block.scalar
        def _(e):
            replay("act", e)

        @block.vector
        def _(e):
            replay("dve", e)

        @block.gpsimd
        def _(e):
            replay("pool", e)


def _consts():
    lg = np.log(1.0 - 2.0 ** (-5.0 - np.arange(H, dtype=np.float64)))
    idx = np.arange(128, dtype=np.float64)
    qdec = np.exp(lg[None, :] * (idx[:, None] + 1.0))
    kinv = np.exp(-lg[None, :] * (idx[:, None] + 1.0)) * (128.0 ** -0.5)
    sdec = np.exp(lg * 128.0)
    gam = np.exp(lg)
    inv = (10000.0 ** (-np.arange(0, 128, 2, dtype=np.float32) / np.float32(128))).astype(np.float32)
    pos = np.arange(SEQ, dtype=np.float32)
    ang = (pos[:, None] * inv[None, :]).astype(np.float32).astype(np.float64)
    ropeP = np.stack([np.cos(ang), np.sin(ang), -np.sin(ang)], axis=1)
    ropeP = ropeP.reshape(NCH, 128, 3, 64).astype(np.float32)
    angs = (np.float32(PAST) * inv).astype(np.float32).astype(np.float64)
    ropeS = np.stack([np.cos(angs), np.sin(angs), -np.sin(angs)], axis=0)[None].repeat(NS, 0).astype(np.float32)
    maskT = (idx[:, None] <= idx[None, :]).astype(np.float32)
    ident = np.eye(128, dtype=np.float32)
    gamP = np.repeat(gam, 16).astype(np.float32).reshape(128, 1)
    blockind = (np.arange(128)[:, None] // 16 == np.arange(H)[None, :]).astype(np.float32)
    return dict(qdec=qdec.astype(np.float32), kinv=kinv.astype(np.float32), sdec=[float(x) for x in sdec],
                gam=[float(x) for x in gam], ropeP=ropeP, ropeS=ropeS, maskT=maskT, ident=ident, gamP=gamP, blockind=blockind)


_C = _consts()


def build_nc():
    nc = bass.Bass("TRN2", target_bir_lowering=False)
    dt_in = lambda name, shape: nc.dram_tensor(name, shape, F32, kind="ExternalInput").ap()
    dt_out = lambda name, shape: nc.dram_tensor(name, shape, F32, kind="ExternalOutput").ap()
    xp = dt_in("xp", [SEQ, D])
    xs_in = dt_in("xs", [NS, D])
    st_in = dt_in("st", [DEPTH, NS, H, 128, 128])
    norm_g = dt_in("norm_g", [DEPTH, D])
    w_in = dt_in("w_in", [DEPTH, D, 7 * D])
    w_out = dt_in("w_out", [DEPTH, 2 * D, D])
    gm_ws = dt_in("gm_ws", [DEPTH, H, 128, 128])
    gm_b = dt_in("gm_b", [DEPTH, H, 128])
    ln_g = dt_in("ln_g", [DEPTH, D])
    ln_b = dt_in("ln_b", [DEPTH, D])
    fin_g = dt_in("fin_g", [D])
    c_qdec = dt_in("c_qdec", [128, H])
    c_kinv = dt_in("c_kinv", [128, H])
    c_ropeP = dt_in("c_ropeP", [NCH, 128, 3, 64])
    c_ropeS = dt_in("c_ropeS", [NS, 3, 64])
    c_maskT = dt_in("c_maskT", [128, 128])
    c_ident = dt_in("c_ident", [128, 128])
    c_gamP = dt_in("c_gamP", [128, 1])
    c_blockind = dt_in("c_blockind", [128, H])
    yp = dt_out("yp", [SEQ, D])
    ys = dt_out("ys", [NS, D])
    sp_out = dt_out("sp_out", [DEPTH, H, 128, 128])
    ss_out = dt_out("ss_out", [DEPTH, NS, H, 128, 128])
    gv_out = dt_out("gv_out", [DEPTH, NS, D])
    h1 = nc.dram_tensor("h1", [SEQ + NS, D], F32, kind="Internal").ap()
    qkvscr = nc.dram_tensor("qkvscr", [3, NS, D], BF16, kind="Internal").ap()
    oscr = nc.dram_tensor("oscr", [NS, D], F32, kind="Internal").ap()
    wbf_in = nc.dram_tensor("wbf_in", [D, 7 * D], BF16, kind="Internal").ap()
    wbf_out = nc.dram_tensor("wbf_out", [2 * D, D], BF16, kind="Internal").ap()

    with ExitStack() as st:
        sb = lambda name, shape, dt: st.enter_context(nc.sbuf_tensor(name, shape, dt))
        wg = [sb("wg%d" % g, [128, 8, 1024], BF16) for g in range(7)]
        wo = [sb("wo%d" % g, [128, 8, 1024], BF16) for g in range(2)]
        identf = sb("identf", [128, 128], F32)
        identb = sb("identb", [128, 128], BF16)
        maskT = sb("maskT", [128, 128], F32)
        qdec = sb("qdec", [128, H], F32)
        kinv = sb("kinv", [128, H], F32)
        gamP = sb("gamP", [128, 1], F32)
        blockind = sb("blockind", [128, H], F32)
        rtab = sb("rtab", [128, 3, 64], F32)
        WT = sb("WT", [128, H, 128], BF16)
        ngT = sb("ngT", [128, 8], F32)
        bsT = sb("bsT", [128, H], F32)
        w00 = sb("w00", [NS, H], F32)
        b00 = sb("b00", [NS, H], F32)
        lng = sb("lng", [128, D], F32)
        lnb = sb("lnb", [128, D], F32)
        fing = sb("fing", [128, D], F32)
        hbuf = [sb("hbuf%d" % i, [128, D], F32) for i in range(3)]
        tmpB = sb("tmpB", [128, D], F32)
        S32 = sb("S32", [128, D], F32)
        xnT = sb("xnT", [128, 8, 128], BF16)
        q_rot = sb("q_rot", [128, D], BF16)
        k_rot = sb("k_rot", [128, D], BF16)
        vb = [sb("vb%d" % i, [128, D], BF16) for i in range(1)]
        vn = sb("vn", [128, D], BF16)
        sg = [sb("sg%d" % i, [128, D], BF16) for i in range(1)]
        sgg = sb("sgg", [128, D], BF16)
        Sbf = sb("Sbf", [128, H, 128], BF16)
        cat = sb("cat", [128, 2 * D], BF16)
        catT = sb("catT", [128, 16, 128], BF16)
        k2all = sb("k2all", [128, NS, 8], BF16)
        q2all = sb("q2all", [128, NS, 8], BF16)
        vb2 = sb("vb2", [128, 8, 128], BF16)
        Q2 = [sb("Q2_%d" % i, [128, 8, H], BF16) for i in range(2)]
        osbt = sb("osbt", [128, 128], F32)
        osb = [osbt[0:H, :]]
        junk = osbt[:, 0:64].bitcast(BF16)
        bst = sb("bst", [128, 12], F32)
        mv = sb("mv", [128, 2], F32)
        oss = sb("oss", [128, H], F32)
        orstd = sb("orstd", [128, H], F32)
        st1 = sb("st1", [128, 8], F32)
        ssq, ssq2, rstd2, rstd, lsd, nmr, epst = [st1[:, i:i + 1] for i in range(7)]
        ps = st.enter_context(nc.psum_tensor("ps", [128, 4, 1024], F32))

        P = Prog(nc)
        from collections import deque
        freep = {0: (-4, 0), 1: (-3, 0), 2: (-2, 0), 3: (-1, 0)}
        atick = [0]

        def alloc():
            while not freep:
                yield "blocked"
            atick[0] += 1
            p = min(freep, key=lambda k: freep[k][0] + freep[k][1])
            del freep[p]
            return p

        def free(p, late=False):
            freep[p] = (atick[0], 2 if late else 0)

        vb2keys = [("vb2", h, g0) for h in range(H) for g0 in range(2)]
        vb2f = vb2[:, :, :].rearrange("p g v -> p (g v)").bitcast(F32)
        qT32 = vb2f[:, 0:256].rearrange("p (h t) -> p h t", h=H)
        kT32 = vb2f[:, 256:512].rearrange("p (h t) -> p h t", h=H)

        def PSb(p):
            return ps[:, p, :].bitcast(BF16)

        xs = tmpB[:, 0:512].bitcast(BF16)
        scT = catT[:, 0:8, :]
        qdT = cat[:, 0:D].rearrange("p (h t) -> p h t", h=H)
        kkT = cat[:, D:2 * D].rearrange("p (h t) -> p h t", h=H)
        catf = cat[:, :].bitcast(F32)
        junk2 = catT[:, 0:8, :].rearrange("p c t -> p (c t)")

        def ld(dst, src, key, eng="sp", **kw):
            P.add(eng, lambda e: e.dma_start(out=dst, in_=src, **kw), w=[key], dma=True, semkey=key)

        ld(identf[:], c_ident, "identf")
        ld(maskT[:], c_maskT, "maskT")
        ld(qdec[:], c_qdec, "qdec")
        ld(kinv[:], c_kinv, "kinv")
        ld(gamP[:], c_gamP, "gamP")
        ld(blockind[:], c_blockind, "blockind")
        P.add("dve", lambda e: e.memset(epst[:], EPS), w=["epst"])
        P.add("dve", lambda e: e.tensor_copy(identb[:], identf[:]), r=["identf"], w=["identb"])

        def precast_weights(l):
            for g in range(7):
                P.add("pool", (lambda g: lambda e: e.dma_start(out=wbf_in[:, g * 1024:(g + 1) * 1024],
                                                               in_=w_in[l, :, g * 1024:(g + 1) * 1024]))(g),
                      w=[("wbf", g)], dma=True, semkey=("wbf", g))
            for g in range(2):
                P.add("pool", (lambda g: lambda e: e.dma_start(out=wbf_out[g * 1024:(g + 1) * 1024, :],
                                                               in_=w_out[l, g * 1024:(g + 1) * 1024, :]))(g),
                      w=[("wbfo", g)], dma=True, semkey=("wbfo", g))

        def load_weights_bf(l, which="all"):
            for g in (range(7) if which in ("all", "in") else []):
                src = wbf_in[:, g * 1024:(g + 1) * 1024].rearrange("(c p) n -> p c n", p=128)
                P.add("pool", (lambda g, src: lambda e: e.dma_start(out=wg[g][:], in_=src))(g, src),
                      r=[("wbf", g)], w=[("wg", g)], dma=True, semkey=("wg", g))
            for g in (range(2) if which in ("all", "out") else []):
                src = wbf_out[g * 1024:(g + 1) * 1024, :].rearrange("(c p) n -> p c n", p=128)
                P.add("pool", (lambda g, src: lambda e: e.dma_start(out=wo[g][:], in_=src))(g, src),
                      r=[("wbfo", g)], w=[("wo", g)], dma=True, semkey=("wo", g))

        def load_weight_group(l, g):
            if g < 7:
                src = w_in[l, :, g * 1024:(g + 1) * 1024].rearrange("(c p) n -> p c n", p=128)
                P.add("pool", lambda e: e.dma_start(out=wg[g][:], in_=src), w=[("wg", g)], dma=True, semkey=("wg", g))
            else:
                src = w_out[l, (g - 7) * 1024:(g - 6) * 1024, :].rearrange("(c p) n -> p c n", p=128)
                P.add("pool", lambda e: e.dma_start(out=wo[g - 7][:], in_=src), w=[("wo", g - 7)], dma=True,
                      semkey=("wo", g - 7))

        deferred_w = []

        def load_weights(l):
            for g in (0, 1, 2):
                load_weight_group(l, g)
            deferred_w.extend([(l, g) for g in (3, 6, 4, 5, 7, 8)])

        def sample_input_loads(l):
            src = xs_in if l == 0 else h1[SEQ:SEQ + NS, :]
            P.add("sp", lambda e: e.dma_start(out=hbuf[0][0:NS, :], in_=src), r=[("h1", "s")] if l else [],
                  w=[("hbuf", 0)], dma=True, semkey=("hbuf", 0))
            P.add("sp", lambda e: e.dma_start(out=rtab[0:NS, :, :], in_=c_ropeS), w=["rtab"], dma=True, semkey="rtab")
            ld(ngT[:], norm_g[l].rearrange("(c p) -> p c", p=128), "ngT", allow_slow_non_contiguous=True)

        def small_loads(l, stg, stgkey):
            ld(bsT[:], gm_b[l].rearrange("h i -> i h"), "bsT", allow_slow_non_contiguous=True)
            ld(w00[:], gm_ws[l, :, 0, 0].partition_broadcast(NS), "w00", allow_slow_non_contiguous=True)
            ld(b00[:], gm_b[l, :, 0].partition_broadcast(NS), "b00", allow_slow_non_contiguous=True)
            ld(lng[:], ln_g[l].partition_broadcast(128), "lng")
            ld(lnb[:], ln_b[l].partition_broadcast(128), "lnb")
            if stg is not None:
                stage_ws(l, stg, stgkey)

        def stage_ws(l, stg, stgkey):
            keys = stgkey if isinstance(stgkey, list) else [stgkey]
            P.add("sp", lambda e: e.dma_start(out=stg[:].rearrange("p (h j) -> p h j", h=H),
                                              in_=gm_ws[l].rearrange("h i j -> i h j")),
                  w=keys, dma=True, semkey=keys[0])

        def wt_prep(l, stg, stgkey):
            p = yield from alloc()

            def tr(e):
                for h in range(H):
                    i = e.transpose(ps[:, p, h * 128:(h + 1) * 128], stg[:, h * 128:(h + 1) * 128], identf[:])
                return i
            yield P.add("pe", tr, r=(stgkey if isinstance(stgkey, list) else [stgkey]) + ["identf"], w=[("ps", p)])
            yield P.add("dve", lambda e: e.tensor_tensor(
                out=WT[:], in0=ps[:, p, :].rearrange("p (h i) -> p h i", h=H),
                in1=maskT[:].unsqueeze(1).to_broadcast([128, H, 128]), op=ALU.mult),
                r=[("ps", p), "maskT"], w=["WT"])
            free(p)

        def rmsnorm_to_xnT(n, hb, hkey, split=False):
            yield P.add("act", lambda e: e.activation(out=xs[0:n, :], in_=hb[0:n, :], func=AF.Square, accum_out=ssq[0:n, :]),
                        r=[hkey], w=["tmpB", "ssq"])
            yield P.add("act", lambda e: e.activation(out=ssq[0:n, :], in_=ssq[0:n, :], func=AF.Sqrt, bias=epst[0:n, :], scale=1.0 / D),
                        r=["ssq", "epst"], w=["ssq"])
            yield P.add("dve", lambda e: e.reciprocal(rstd[0:n, :], ssq[0:n, :]), r=["ssq"], w=["rstd"])
            yield P.add("act", lambda e: e.activation(out=xs[0:n, :], in_=hb[0:n, :], func=AF.Copy, scale=rstd[0:n, :]),
                        r=[hkey, "rstd"], w=["tmpB"])
            if split:
                yield "seg"
            p = yield from alloc()
            pb = PSb(p)

            def tr(e):
                for c in range(8):
                    i = e.transpose(pb[:, c * 128:c * 128 + n], xs[0:n, c * 128:(c + 1) * 128], identb[0:n, 0:n])
                return i
            yield P.add("pe", tr, r=["tmpB", "identb"], w=[("ps", p)])
            yield P.add("dve", lambda e: e.tensor_tensor(
                out=xnT[:, :, 0:n], in0=pb[:, 0:1024].rearrange("p (c t) -> p c t", c=8)[:, :, 0:n],
                in1=ngT[:].unsqueeze(2).to_broadcast([128, 8, n]), op=ALU.mult),
                r=[("ps", p), "ngT"], w=["xnT"])
            free(p)

        def proj(g, n):
            p = yield from alloc()

            def mm(e):
                for half in range(2):
                    for c in range(8):
                        i = e.matmul(ps[0:n, p, half * 512:(half + 1) * 512], xnT[:, c, 0:n],
                                     wg[g][:, c, half * 512:(half + 1) * 512], start=(c == 0), stop=(c == 7))
                return i
            yield P.add("pe", mm, r=["xnT", ("wg", g)], w=[("ps", p)])
            return p

        def rope(p, n, dst, dstkey, dst32=None):
            src16 = ps[0:n, p, :].rearrange("p (a d) -> p a d", d=64)
            src4 = ps[0:n, p, :].rearrange("p (h t d) -> p h t d", h=H, t=2)
            t4 = tmpB[0:n, :].rearrange("p (h t d) -> p h t d", h=H, t=2)

            def t2(e):
                e.tensor_tensor(out=t4[:, :, 0, :], in0=src4[:, :, 1, :],
                                in1=rtab[0:n, 2, :].unsqueeze(1).to_broadcast([n, H, 64]), op=ALU.mult)
                return e.tensor_tensor(out=t4[:, :, 1, :], in0=src4[:, :, 0, :],
                                       in1=rtab[0:n, 1, :].unsqueeze(1).to_broadcast([n, H, 64]), op=ALU.mult)
            yield P.add("dve", t2, r=[("ps", p), "rtab"], w=["tmpB"])
            yield P.add("dve", lambda e: e.tensor_tensor(
                out=src16, in0=src16, in1=rtab[0:n, 0, :].unsqueeze(1).to_broadcast([n, 16, 64]), op=ALU.mult),
                r=[("ps", p), "rtab"], w=[("ps", p)])
            yield P.add("dve", lambda e: e.tensor_tensor(out=dst[0:n, :], in0=ps[0:n, p, :], in1=tmpB[0:n, :], op=ALU.add),
                        r=[("ps", p), "tmpB"], w=[dstkey])
            if dst32 is not None:
                yield P.add("dve", lambda e: e.tensor_tensor(out=tmpB[0:32, :], in0=ps[0:32, p, :], in1=tmpB[0:32, :], op=ALU.add),
                            r=[("ps", p), "tmpB"], w=["tmpB"])
                px = yield from alloc()

                def tr32(e):
                    for h in range(H):
                        i = e.transpose(ps[:, px, h * 32:(h + 1) * 32], tmpB[0:32, h * 128:(h + 1) * 128], identf[0:32, 0:32])
                    return i
                yield P.add("pe", tr32, r=["tmpB", "identf"], w=[("ps", px)])
                yield P.add("act", lambda e: e.activation(out=dst32, in_=ps[:, px, 0:256].rearrange("p (h t) -> p h t", h=H),
                                                          func=AF.Copy), r=[("ps", px)], w=vb2keys)
                free(px)
            free(p, late=True)

        def transpose_heads(src, srckey, n, dst, dstkey):
            p = yield from alloc()
            pb = PSb(p)

            def tr(e):
                for h in range(H):
                    i = e.transpose(pb[:, h * 128:h * 128 + n], src[0:n, h * 128:(h + 1) * 128], identb[0:n, 0:n])
                return i
            yield P.add("pe", tr, r=[srckey, "identb"], w=[("ps", p)])
            yield P.add("act", lambda e: e.activation(
                out=dst[:, :, 0:n], in_=pb[:, 0:1024].rearrange("p (h t) -> p h t", h=H)[:, :, 0:n], func=AF.Copy),
                r=[("ps", p)], w=[dstkey])
            free(p)

        def layernorm_vg(p, n, sample, l):
            def stats(e):
                e.bn_stats(bst[0:n, 0:6], ps[0:n, p, 0:512])
                return e.bn_stats(bst[0:n, 6:12], ps[0:n, p, 512:1024])
            yield P.add("dve", stats, r=[("ps", p)], w=["bst"])
            yield P.add("dve", lambda e: e.bn_aggr(mv[0:n, :], bst[0:n, :]), r=["bst"], w=["mv"])
            yield P.add("act", lambda e: e.activation(out=lsd[0:n, :], in_=mv[0:n, 1:2], func=AF.Sqrt, bias=epst[0:n, :], scale=1.0),
                        r=["mv", "epst"], w=["lsd"])
            yield P.add("dve", lambda e: e.reciprocal(lsd[0:n, :], lsd[0:n, :]), r=["lsd"], w=["lsd"])
            yield P.add("dve", lambda e: e.tensor_scalar(out=nmr[0:n, :], in0=mv[0:n, 0:1], scalar1=lsd[0:n, 0:1], scalar2=-1.0,
                                                         op0=ALU.mult, op1=ALU.mult), r=["mv", "lsd"], w=["nmr"])
            yield P.add("act", lambda e: e.activation(out=tmpB[0:n, :], in_=ps[0:n, p, :], func=AF.Identity,
                                                      bias=nmr[0:n, :], scale=lsd[0:n, :]),
                        r=[("ps", p), "nmr", "lsd"], w=["tmpB"])
            free(p)
            yield P.add("pool", lambda e: e.tensor_tensor(out=tmpB[0:n, :], in0=tmpB[0:n, :], in1=lng[0:n, :], op=ALU.mult),
                        r=["tmpB", "lng"], w=["tmpB"])
            if sample:
                yield P.add("pool", lambda e: e.tensor_tensor(out=tmpB[0:n, :], in0=tmpB[0:n, :], in1=lnb[0:n, :], op=ALU.add),
                            r=["tmpB", "lnb"], w=["tmpB"])
                yield P.add("sp", lambda e: e.dma_start(out=gv_out[l], in_=tmpB[0:n, :]), r=["tmpB"], w=[("gv", l)],
                            dma=True, semkey="tmpB")
            else:
                yield P.add("pool", lambda e: e.tensor_tensor(out=vn[0:n, :], in0=tmpB[0:n, :], in1=lnb[0:n, :], op=ALU.add),
                            r=["tmpB", "lnb"], w=["vn"])

        def groupnorm_gate(po, n, sgt, sgkey, src=None, srckey=None):
            if src is None:
                src = lambda h: ps[0:n, po, h * 128:(h + 1) * 128]
                srckey = ("ps", po)

            def sq(e):
                for h in range(H):
                    i = e.activation(out=junk[0:n, :], in_=src(h), func=AF.Square,
                                     accum_out=oss[0:n, h:h + 1])
                return i
            yield P.add("act", sq, r=[srckey], w=["oss", ("osb", 0)])
            yield P.add("act", lambda e: e.activation(out=oss[0:n, :], in_=oss[0:n, :], func=AF.Sqrt, bias=epst[0:n, :], scale=1.0 / 128),
                        r=["oss", "epst"], w=["oss"])
            yield P.add("dve", lambda e: e.reciprocal(orstd[0:n, :], oss[0:n, :]), r=["oss"], w=["orstd"])

            def gate(e):
                for h in range(H):
                    i = e.scalar_tensor_tensor(out=cat[0:n, h * 128:(h + 1) * 128], in0=src(h),
                                               scalar=orstd[0:n, h:h + 1], in1=sgt[0:n, h * 128:(h + 1) * 128],
                                               op0=ALU.mult, op1=ALU.mult)
                return i
            yield P.add("dve", gate, r=[srckey, "orstd", sgkey], w=["cat_o"])
            if po is not None:
                free(po, late=True)

        def out_proj_residual(n, hb, hkey, seg=False):
            p = yield from alloc()
            pb = PSb(p)

            def tr(e):
                for c in range(16):
                    i = e.transpose(pb[:, c * 128:c * 128 + n], cat[0:n, c * 128:(c + 1) * 128], identb[0:n, 0:n])
                return i
            yield P.add("pe", tr, r=["cat_o", "cat_m", "identb"], w=[("ps", p)])
            yield P.add("act", lambda e: e.activation(
                out=catT[:, :, 0:n], in_=pb[:, :].rearrange("p (c t) -> p c t", c=16)[:, :, 0:n], func=AF.Copy),
                r=[("ps", p)], w=["catT"])
            free(p)
            if seg:
                yield "seg"
            py = yield from alloc()

            def mm(e):
                for half in range(2):
                    for c in range(16):
                        i = e.matmul(ps[0:n, py, half * 512:(half + 1) * 512], catT[:, c, 0:n],
                                     wo[c // 8][:, c % 8, half * 512:(half + 1) * 512], start=(c == 0), stop=(c == 15))
                return i
            yield P.add("pe", mm, r=["catT", ("wo", 0), ("wo", 1)], w=[("ps", py)])
            yield P.add("dve", lambda e: e.tensor_tensor(out=hb[0:n, :], in0=ps[0:n, py, :], in1=hb[0:n, :], op=ALU.add),
                        r=[("ps", py), hkey], w=[hkey])
            free(py)

        def final_norm_store(n, hb, hkey, dst, dstkey):
            yield P.add("act", lambda e: e.activation(out=junk2[0:n, :], in_=hb[0:n, :], func=AF.Square, accum_out=ssq2[0:n, :]),
                        r=[hkey], w=["catT", "ssq2"])
            yield P.add("act", lambda e: e.activation(out=ssq2[0:n, :], in_=ssq2[0:n, :], func=AF.Sqrt, bias=epst[0:n, :], scale=1.0 / D),
                        r=["ssq2", "epst"], w=["ssq2"])
            yield P.add("dve", lambda e: e.reciprocal(rstd2[0:n, :], ssq2[0:n, :]), r=["ssq2"], w=["rstd2"])
            yield P.add("dve", lambda e: e.scalar_tensor_tensor(out=hb[0:n, :], in0=hb[0:n, :], scalar=rstd2[0:n, 0:1],
                                                                in1=fing[0:n, :], op0=ALU.mult, op1=ALU.mult),
                        r=[hkey, "rstd2", "fing"], w=[hkey])
            yield P.add("pool", lambda e: e.dma_start(out=dst, in_=hb[0:n, :]), r=[hkey], w=[dstkey], dma=True, semkey=hkey)

        ver = {}

        def setv(key, c, sample):
            ver[key] = ("s" if sample else c)

        def chk(keys, c):
            for k in keys:
                assert ver.get(k) == c, ("stale/early buffer", k, ver.get(k), c)

        def bufs(c, sample):
            par = 0 if sample else (c + 1) % 3
            return (hbuf[par], ("hbuf", par), vb[0], ("vb", 0), sg[0], ("sg", 0))

        def stageA(l, c, sample, part=0):
            n = NS if sample else 128
            hb, hkey, vbt, vkey, sgt, sgkey = bufs(c, sample)
            if l == 0:
                src = xs_in if sample else xp[c * 128:(c + 1) * 128, :]
            else:
                src = h1[SEQ:SEQ + NS, :] if sample else h1[c * 128:(c + 1) * 128, :]
            srckey = ("h1", "s" if sample else c)

            def seg_norm(cc=c):
                hb2, hkey2 = bufs(cc, sample)[0:2]
                if l == 0:
                    src2 = xs_in if sample else xp[cc * 128:(cc + 1) * 128, :]
                else:
                    src2 = h1[SEQ:SEQ + NS, :] if sample else h1[cc * 128:(cc + 1) * 128, :]
                srckey2 = ("h1", "s" if sample else cc)
                if not sample:
                    yield P.add("sp", lambda e: e.dma_start(out=hb2[0:n, :], in_=src2), r=[srckey2] if l else [], w=[hkey2],
                                dma=True, semkey=hkey2)
                    rsrc = c_ropeS if sample else c_ropeP[cc]
                    yield P.add("sp", lambda e: e.dma_start(out=rtab[0:n, :, :], in_=rsrc), w=["rtab"], dma=True, semkey="rtab")
                yield from rmsnorm_to_xnT(n, hb2, hkey2, split=(part == 0 and cc != c))

            def seg_q():
                pq = yield from proj(0, n)
                if not sample:
                    def presq(e):
                        for h in range(H):
                            i = e.activation(out=ps[0:n, pq, h * 128:(h + 1) * 128], in_=ps[0:n, pq, h * 128:(h + 1) * 128],
                                             func=AF.Copy, scale=qdec[0:n, h:h + 1])
                        return i
                    yield P.add("act", presq, r=[("ps", pq), "qdec"], w=[("ps", pq)])
                if part == 0:
                    yield "seg"
                yield from rope(pq, n, q_rot, "q_rot", dst32=(qT32 if (not sample and c == 0) else None))
                setv("q_rot", c, sample)

            def seg_k():
                pk = yield from proj(1, n)
                if sample:
                    yield P.add("act", lambda e: e.activation(out=ps[0:n, pk, :], in_=ps[0:n, pk, :], func=AF.Copy, scale=128.0 ** -0.5),
                                r=[("ps", pk)], w=[("ps", pk)])
                else:
                    def presk(e):
                        for h in range(H):
                            i = e.activation(out=ps[0:n, pk, h * 128:(h + 1) * 128], in_=ps[0:n, pk, h * 128:(h + 1) * 128],
                                             func=AF.Copy, scale=kinv[0:n, h:h + 1])
                        return i
                    yield P.add("act", presk, r=[("ps", pk), "kinv"], w=[("ps", pk)])
                if part == 0:
                    yield "seg"
                yield from rope(pk, n, k_rot, "k_rot", dst32=(kT32 if (not sample and c == 0) else None))
                setv("k_rot", c, sample)

            def seg_v():
                pv = yield from proj(2, n)
                yield P.add("act", lambda e: e.activation(out=vbt[0:n, :], in_=ps[0:n, pv, :], func=AF.Copy), r=[("ps", pv)], w=[vkey])
                free(pv)
                setv(vkey, c, sample)

            def seg_gr():
                pg = yield from proj(3, n)
                yield P.add("act", lambda e: e.activation(out=sgt[0:n, :], in_=ps[0:n, pg, :], func=AF.Silu), r=[("ps", pg)], w=[sgkey])
                free(pg)
                setv(sgkey, c, sample)

            def seg_gg():
                pgg = yield from proj(6, n)
                yield P.add("act", lambda e: e.activation(out=sgg[0:n, :], in_=ps[0:n, pgg, :], func=AF.Silu), r=[("ps", pgg)], w=["sgg"])
                free(pgg)

            def seg_u():
                pu = yield from proj(4, n)
                yield P.add("dve", lambda e: e.tensor_tensor(out=sgg[0:n, :], in0=ps[0:n, pu, :], in1=sgg[0:n, :], op=ALU.mult),
                            r=[("ps", pu), "sgg"], w=["sgg"])
                free(pu)
                setv("sgg", c, sample)

            def seg_vg():
                pvg = yield from proj(5, n)
                yield from layernorm_vg(pvg, n, sample, l)
                setv("vn", c, sample)

            if part == 0:
                segs = [seg_q, seg_k, seg_v, seg_vg, seg_gr, seg_gg]
                if c + 1 < NCH:
                    nrm = seg_norm(c + 1)

                    def nrm_a():
                        for r in nrm:
                            if r == "seg":
                                return
                            yield r

                    def nrm_b():
                        yield from nrm
                    segs += [nrm_a, seg_u, nrm_b]
                else:
                    segs.append(seg_u)
            elif part == 3:
                segs = [seg_norm]
            elif part == 1:
                segs = [seg_norm, seg_q, seg_k, seg_v]
            else:
                segs = [seg_gr, seg_gg, seg_u, seg_vg]
            for i, sgm in enumerate(segs):
                yield from sgm()
                if i + 1 < len(segs):
                    yield "seg"

        def stageB(l, c):
            n = 128
            hb, hkey, vbt, vkey, sgt, sgkey = bufs(c, False)
            ptq = yield from alloc()
            pbq = PSb(ptq)

            def trqk(e):
                for h in range(H):
                    i = e.transpose(pbq[:, h * 128:(h + 1) * 128], q_rot[:, h * 128:(h + 1) * 128], identb[:])
                for h in range(H):
                    i = e.transpose(pbq[:, D + h * 128:D + (h + 1) * 128], k_rot[:, h * 128:(h + 1) * 128], identb[:])
                return i
            chk(["q_rot", "k_rot"], c)
            yield P.add("pe", trqk, r=["q_rot", "k_rot", "identb"], w=[("ps", ptq)])
            yield P.add("act", lambda e: e.activation(out=cat[:, :], in_=pbq[:, :], func=AF.Copy),
                        r=[("ps", ptq)], w=["cat_o", "cat_m"])
            free(ptq)
            pkv = yield from alloc()

            def mm_kv(e):
                for h in range(H):
                    i = e.matmul(ps[:, pkv, h * 128:(h + 1) * 128], k_rot[:, h * 128:(h + 1) * 128],
                                 vbt[:, h * 128:(h + 1) * 128], start=True, stop=True)
                return i
            chk(["k_rot", vkey], c)
            yield P.add("pe", mm_kv, r=["k_rot", vkey], w=[("ps", pkv)])
            if c == 0:
                yield P.add("dve", lambda e: e.tensor_copy(S32[:], ps[:, pkv, :]), r=[("ps", pkv)], w=["S32"])
            else:
                def upd(e):
                    for h in range(H):
                        i = e.scalar_tensor_tensor(out=S32[:, h * 128:(h + 1) * 128], in0=S32[:, h * 128:(h + 1) * 128],
                                                   scalar=_C["sdec"][h], in1=ps[:, pkv, h * 128:(h + 1) * 128],
                                                   op0=ALU.mult, op1=ALU.add)
                    return i
                yield P.add("dve", upd, r=[("ps", pkv), "S32"], w=["S32"])
            free(pkv)
            yield "seg"
            psc = yield from alloc()

            def mm_sc(e):
                for h in range(H):
                    i = e.matmul(ps[:, psc, h * 128:(h + 1) * 128], kkT[:, h, :], qdT[:, h, :], start=True, stop=True)
                    if c == 0:
                        i = e.matmul(ps[0:32, psc, h * 128:h * 128 + 32], kT32[:, h, :], qT32[:, h, :], start=True, stop=True)
                return i
            yield P.add("pe", mm_sc, r=["cat_m", "cat_o"] + (vb2keys if c == 0 else []), w=[("ps", psc)])
            yield P.add("dve", lambda e: e.tensor_tensor(
                out=scT, in0=ps[:, psc, :].rearrange("p (h i) -> p h i", h=H),
                in1=maskT[:].unsqueeze(1).to_broadcast([128, H, 128]), op=ALU.mult),
                r=[("ps", psc), "maskT"], w=["catT"])
            free(psc)
            yield "seg"
            po = yield from alloc()

            def mm_o(e):
                for h in range(H):
                    i = e.matmul(ps[:, po, h * 128:(h + 1) * 128], scT[:, h, :], vbt[:, h * 128:(h + 1) * 128],
                                 start=True, stop=(c == 0))
                    if c > 0:
                        i = e.matmul(ps[:, po, h * 128:(h + 1) * 128], qdT[:, h, :], Sbf[:, h, :], start=False, stop=True)
                return i
            chk([vkey], c)
            yield P.add("pe", mm_o, r=["catT", vkey, "cat_o"] + (["Sbf"] if c > 0 else []), w=[("ps", po)])
            yield from groupnorm_gate(po, n, sgt, sgkey)
            yield "seg"
            pgm = yield from alloc()

            def mm_g(e):
                for h in range(H):
                    i = e.matmul(ps[:, pgm, h * 128:(h + 1) * 128], WT[:, h, :], vn[:, h * 128:(h + 1) * 128],
                                 start=True, stop=True)
                return i
            chk(["vn", "sgg", sgkey], c)
            yield P.add("pe", mm_g, r=["WT", "vn"], w=[("ps", pgm)])

            def gm(e):
                for h in range(H):
                    i = e.scalar_tensor_tensor(out=cat[0:n, D + h * 128:D + (h + 1) * 128],
                                               in0=ps[0:n, pgm, h * 128:(h + 1) * 128], scalar=bsT[0:n, h:h + 1],
                                               in1=sgg[0:n, h * 128:(h + 1) * 128], op0=ALU.add, op1=ALU.mult)
                return i
            yield P.add("dve", gm, r=[("ps", pgm), "bsT", "sgg"], w=["cat_m"])
            free(pgm, late=True)
            yield "seg"
            yield from out_proj_residual(n, hb, hkey, seg=True)
            if l == 0:
                dst = h1[c * 128:(c + 1) * 128, :]
                yield P.add("pool", lambda e: e.dma_start(out=dst, in_=hb[0:n, :]), r=[hkey], w=[("h1", c)], dma=True, semkey=hkey)
            else:
                yield from final_norm_store(n, hb, hkey, yp[c * 128:(c + 1) * 128, :], ("yp", c))
            if c < NCH - 1:
                def sbf(e):
                    for h in range(H):
                        i = e.activation(out=Sbf[:, h, :], in_=S32[:, h * 128:(h + 1) * 128], func=AF.Copy,
                                         scale=_C["sdec"][h])
                    return i
                yield P.add("act", sbf, r=["S32"], w=["Sbf"])
            else:
                def sfin(e):
                    for h in range(H):
                        i = e.activation(out=S32[:, h * 128:(h + 1) * 128], in_=S32[:, h * 128:(h + 1) * 128],
                                         func=AF.Copy, scale=_C["sdec"][h])
                    return i
                yield P.add("act", sfin, r=["S32"], w=["S32"])
                yield P.add("sp", lambda e: e.dma_start(out=sp_out[l].rearrange("h d v -> d h v"),
                                                        in_=S32[:].rearrange("p (h v) -> p h v", h=H)),
                            r=["S32"], w=[("spo", l)], dma=True, semkey="S32")

        def sample_relayout(l):
            n = NS
            hb, hkey, vbt, vkey, sgt, sgkey = bufs(0, True)
            yield P.add("sp", lambda e: e.dma_start(out=qkvscr[2], in_=vbt[0:n, :]), r=[vkey], w=[("scr", 2)],
                        dma=True, semkey=vkey)
            for t, key, dst, dkey in [(q_rot, "q_rot", q2all, "q2all"), (k_rot, "k_rot", k2all, "k2all")]:
                p = yield from alloc()
                pb = PSb(p)
                tv8 = t[0:n, :].rearrange("b (p dl) -> b dl p", dl=8)

                def tr(e, pb=pb, tv8=tv8):
                    for dl in range(8):
                        i = e.transpose(pb[:, dl * NS:(dl + 1) * NS], tv8[:, dl, :], identb[0:n, 0:n])
                    return i
                yield P.add("pe", tr, r=[key, "identb"], w=[("ps", p)])
                yield P.add("act", (lambda pb, dst: lambda e: e.activation(
                    out=dst[:].rearrange("p b dl -> p dl b"), in_=pb[:, 0:8 * NS].rearrange("p (dl b) -> p dl b", dl=8),
                    func=AF.Copy))(pb, dst), r=[("ps", p)], w=[dkey])
                free(p)

        def sample_loop(l):
            n = NS
            hb, hkey, vbt, vkey, sgt, sgkey = bufs(0, True)
            stbuf = [(S32, "S32"), (hbuf[1], ("hbuf", 1)), (hbuf[2], ("hbuf", 2)), (tmpB, "tmpB")]

            def front(b):
                Sb, Skey = stbuf[b % 4]
                pre = (l, b) in preloaded
                if b % 4 == 0:
                    g0 = (b // 4) % 2
                    for h in range(H):
                        yield P.add("sp", (lambda h: lambda e: e.dma_start(
                            out=vb2[h * 16:(h + 1) * 16, g0 * 4:(g0 + 1) * 4, :],
                            in_=qkvscr[2, b:b + 4, h * 128:(h + 1) * 128].partition_broadcast(16)))(h),
                            r=[("scr", 2)], w=[("vb2", h, g0)], dma=True, semkey=("vb2", h, g0))
                if not pre:
                    yield P.add("sp", lambda e: e.dma_start(
                        out=Sb[:], in_=st_in[l, b].rearrange("h (dh dl) v -> (h dh) (dl v)", dl=8)),
                        w=[Skey], dma=True, semkey=Skey)
                yield P.add("act", lambda e: e.activation(out=Sb[:], in_=Sb[:], func=AF.Copy, scale=gamP[:, 0:1]),
                            r=[Skey, "gamP"], w=[Skey])

                def upd(e):
                    for dl in range(8):
                        i = e.scalar_tensor_tensor(out=Sb[:, dl * 128:(dl + 1) * 128], in0=vb2[:, b % 8, :],
                                                   scalar=k2all[:, b, dl:dl + 1], in1=Sb[:, dl * 128:(dl + 1) * 128],
                                                   op0=ALU.mult, op1=ALU.add)
                    return i
                yield P.add("dve", upd, r=[Skey, "k2all"] + [("vb2", h, (b // 4) % 2) for h in range(H)], w=[Skey])
                qm = Q2[b % 2]
                yield P.add("dve", lambda e: e.tensor_tensor(
                    out=qm[:], in0=q2all[:, b, :].unsqueeze(2).to_broadcast([128, 8, H]),
                    in1=blockind[:].unsqueeze(1).to_broadcast([128, 8, H]), op=ALU.mult),
                    r=["q2all", "blockind"], w=[("Q2", b % 2)])

            pobs = {}

            def back(b):
                Sb, Skey = stbuf[b % 4]
                qm = Q2[b % 2]
                qmkey = ("Q2", b % 2)
                Sbv = Sbf[:].rearrange("p h v -> p (h v)")
                yield P.add("act", lambda e: e.activation(out=Sbv, in_=Sb[:], func=AF.Copy), r=[Skey], w=["Sbf"])
                yield P.add("act", lambda e: e.dma_start(
                    out=ss_out[l, b].rearrange("h (dh dl) v -> (h dh) (dl v)", dl=8), in_=Sb[:]),
                    r=[Skey], w=[("sso", l, b)], dma=True, semkey=Skey)
                pob = yield from alloc()

                def mm_os(e):
                    for dl in range(8):
                        i = e.matmul(ps[0:H, pob, 0:128], qm[:, dl, :], Sbv[:, dl * 128:(dl + 1) * 128],
                                     start=(dl == 0), stop=(dl == 7))
                    return i
                yield P.add("pe", mm_os, r=[qmkey, "Sbf"], w=[("ps", pob)])
                pobs[b] = pob

            def tail(b):
                pob = pobs.pop(b)
                ob = osb[0]
                obkey = ("osb", 0)
                yield P.add("act", lambda e: e.activation(out=ob, in_=ps[0:H, pob, 0:128], func=AF.Copy),
                            r=[("ps", pob)], w=[obkey])
                free(pob)
                yield P.add("act", lambda e: e.dma_start(out=oscr[b].rearrange("(h v) -> h v", h=H), in_=ob),
                            r=[obkey], w=[("oscr", b)], dma=True, semkey=obkey)

            yield from front(0)
            yield from front(1)
            for b in range(NS):
                yield from back(b)
                if b >= 1:
                    yield from tail(b - 1)
                if b + 2 < NS:
                    yield from front(b + 2)
                if deferred_w and b % 2 == 1:
                    load_weight_group(*deferred_w.pop(0))
            yield from tail(NS - 1)
            while deferred_w:
                load_weight_group(*deferred_w.pop(0))
            yield P.add("sp", lambda e: e.dma_start(out=hbuf[2][0:n, :], in_=oscr),
                        r=[("oscr", b) for b in range(NS)], w=[("hbuf", 2)], dma=True, semkey=("hbuf", 2))

        def sample_stream(l):
            n = NS
            hb, hkey, vbt, vkey, sgt, sgkey = bufs(0, True)
            yield from stageA(l, 0, True, part=1)
            yield from sample_relayout(l)
            if l != 0:
                yield from wt_prep(l, catf, ["cat_o", "cat_m"])
            if l == 0:
                yield from sample_loop(l)
            yield from stageA(l, 0, True, part=2)
            tv = tmpB[0:n, :].rearrange("p (h g) -> p h g", h=H)
            yield P.add("dve", lambda e: e.tensor_tensor(out=tv, in0=tv, in1=w00[:].unsqueeze(2).to_broadcast([n, H, 128]),
                                                         op=ALU.mult), r=["tmpB", "w00"], w=["tmpB"])
            yield P.add("dve", lambda e: e.tensor_tensor(out=tv, in0=tv, in1=b00[:].unsqueeze(2).to_broadcast([n, H, 128]),
                                                         op=ALU.add), r=["tmpB", "b00"], w=["tmpB"])
            yield P.add("dve", lambda e: e.tensor_tensor(out=cat[0:n, D:2 * D], in0=tmpB[0:n, :], in1=sgg[0:n, :], op=ALU.mult),
                        r=["tmpB", "sgg"], w=["cat_m"])
            if l != 0:
                yield from sample_loop(l)

        def sample_tail(l):
            n = NS
            hb, hkey, vbt, vkey, sgt, sgkey = bufs(0, True)
            yield from groupnorm_gate(None, n, sgt, sgkey, src=lambda h: hbuf[2][0:n, h * 128:(h + 1) * 128],
                                      srckey=("hbuf", 2))
            yield "seg"
            yield from out_proj_residual(n, hb, hkey, seg=True)
            if l == 0:
                yield P.add("sp", lambda e: e.dma_start(out=h1[SEQ:SEQ + NS, :], in_=hb[0:n, :]), r=[hkey], w=[("h1", "s")],
                            dma=True, semkey=hkey)
            else:
                yield from final_norm_store(n, hb, hkey, ys, ("ys",))

        def drive(streams):
            prog = [0.0] * len(streams)
            alive = [True] * len(streams)
            while any(alive):
                order = sorted([i for i in range(len(streams)) if alive[i]], key=lambda i: prog[i])
                stepped = False
                for i in order:
                    try:
                        r = next(streams[i][0])
                    except StopIteration:
                        alive[i] = False
                        stepped = True
                        break
                    if r == "blocked":
                        continue
                    if r == "seg":
                        stepped = True
                        break
                    prog[i] += 1.0 / streams[i][1]
                    stepped = True
                    break
                assert stepped, "all streams blocked on PSUM allocation"

        def drive_script(ga, gb, pattern):
            gens = {"A": ga, "B": gb}
            done = {"A": False, "B": False}
            for who in pattern:
                if done[who]:
                    continue
                while True:
                    try:
                        r = next(gens[who])
                    except StopIteration:
                        done[who] = True
                        break
                    assert r != "blocked", "PSUM alloc blocked in scripted merge"
                    if r == "seg":
                        break
            for who in ("A", "B"):
                if not done[who]:
                    for r in gens[who]:
                        assert r != "blocked"

        preloaded = set()

        def preload_states(l):
                for b, (Sb, Skey) in enumerate([(S32, "S32"), (hbuf[1], ("hbuf", 1)), (hbuf[2], ("hbuf", 2))]):
                    P.add("sp", (lambda Sb, b, l: lambda e: e.dma_start(
                        out=Sb[:], in_=st_in[l, b].rearrange("h (dh dl) v -> (h dh) (dl v)", dl=8)))(Sb, b, l),
                        w=[Skey], dma=True, semkey=Skey)
                    preloaded.add((l, b))

        for l in range(DEPTH):
            if l == 0:
                sample_input_loads(l)
                preload_states(l)
                small_loads(l, fing, "fing")
                load_weights(l)
            else:
                small_loads(l, None, None)
                preload_states(l)
            drive([[sample_stream(l), 1]])
            if l == 0:
                drive([[wt_prep(l, fing, "fing"), 1]])
                ld(fing[:], fin_g.partition_broadcast(128), "fing")

            def first_A(l=l):
                yield from stageA(l, 0, False, part=3)
                yield "seg"
                yield from stageA(l, 0, False)
            drive_script(first_A(), sample_tail(l), ["B", "A", "A", "A", "B", "A", "A", "B"] + ["A"] * 12)
            for c in range(NCH):
                if l == 0 and c == 2:
                    precast_weights(1)
                if c + 1 < NCH:
                    drive_script(stageA(l, c + 1, False), stageB(l, c), PATTERN)
                else:
                    if l + 1 < DEPTH:
                        sample_input_loads(l + 1)
                        load_weights_bf(l + 1, "in")
                    drive([[stageB(l, c), 1]])
                    if l + 1 < DEPTH:
                        load_weights_bf(l + 1, "out")
                        stage_ws(l + 1, catf, ["cat_o", "cat_m"])
        outkeys = [("ys",)] + [("yp", c) for c in range(NCH)] + [("spo", l) for l in range(DEPTH)] + \
                  [("sso", l, b) for l in range(DEPTH) for b in range(NS)] + [("gv", l) for l in range(DEPTH)]
        P.add("sp", lambda e: e.nop(), r=outkeys)
        P.emit(st)
        nc._prog = P
    return nc


PATTERN = ["B", "A", "B", "A", "B", "A", "A", "A", "B", "A", "A", "B", "A", "A", "B", "A", "A"]

_NC_CACHE = {}


def kernel(x_prompt, x_sample, state_ret, norm_g, w_in, w_out, gm_ws, gm_b, gm_ln_g, gm_ln_b, final_g):
    f = lambda a: np.ascontiguousarray(np.asarray(a, dtype=np.float32))
    x_prompt, x_sample, state_ret = f(x_prompt), f(x_sample), f(state_ret)
    shared = dict(norm_g=f(norm_g), w_in=f(w_in), w_out=f(w_out), gm_ws=f(gm_ws), gm_b=f(gm_b), ln_g=f(gm_ln_g),
                  ln_b=f(gm_ln_b), fin_g=f(final_g), c_qdec=_C["qdec"], c_kinv=_C["kinv"], c_ropeP=_C["ropeP"],
                  c_ropeS=_C["ropeS"], c_maskT=_C["maskT"], c_ident=_C["ident"], c_gamP=_C["gamP"], c_blockind=_C["blockind"])
    in_maps = []
    for c in range(NCORES):
        m = dict(shared)
        m["xp"] = x_prompt[c]
        m["xs"] = np.ascontiguousarray(x_sample[c * NS:(c + 1) * NS, 0, :])
        m["st"] = np.ascontiguousarray(state_ret[:, c * NS:(c + 1) * NS])
        in_maps.append(m)
    if "nc" not in _NC_CACHE:
        _NC_CACHE["nc"] = build_nc()
    res = run_bass_kernel_spmd(_NC_CACHE["nc"], in_maps, core_ids=list(range(NCORES)))
    R = res.results
    y_prompt = np.stack([R[c]["yp"] for c in range(NCORES)], 0)
    y_sample = np.concatenate([R[c]["ys"] for c in range(NCORES)], 0)[:, None, :]
    sp = np.stack([R[c]["sp_out"] for c in range(NCORES)], 1)
    ss = np.concatenate([R[c]["ss_out"] for c in range(NCORES)], 1)
    gv = np.concatenate([R[c]["gv_out"] for c in range(NCORES)], 1).reshape(DEPTH, NCORES * NS, 1, H, 128)
    return (y_prompt.astype(np.float32), y_sample.astype(np.float32), sp.astype(np.float32),
            ss.astype(np.float32), gv.astype(np.float32))
```
